# Optimizing a Trainium2 kernel written in Bass

```python
import jax, jax.numpy as jnp
from jax import lax
import numpy as np

D_MODEL = 1024
BATCH = 16
SEQ = 4096
DEPTH = 2
DEC_BATCH = 8
DEC_SEQ = 32
PAST_LEN = 1024

CHUNK = 64
D_MIX = D_MODEL
W_GROUP = D_MIX // 4
POOL_WINDOWS = (2, 4, 8, 16)
N_POOL_GROUPS = len(POOL_WINDOWS)
POOL_GC = W_GROUP // N_POOL_GROUPS
POOL_HIST = max(POOL_WINDOWS) - 1
CONF_K = 31
SCONV_K = 3
MLP_CHUNK = 128
MLP_HEADS = 4
MLP_HD = W_GROUP // MLP_HEADS
D_IN = 8 * W_GROUP
D_FF = ((-(-8 * D_MODEL // 3) + 255) // 256) * 256
EPS = 1e-6

kernel_name = "hybrid_streaming_encoder_step"


def rms_norm(x, g):
    xf = x.astype(jnp.float32)
    y = xf * lax.rsqrt(jnp.mean(xf * xf, axis=-1, keepdims=True) + EPS)
    return (y * g.astype(jnp.float32)).astype(x.dtype)


def causal_depthwise(xh, w):
    return lax.conv_general_dilated(
        xh, w[:, None, :].astype(xh.dtype), (1,), 'VALID',
        dimension_numbers=('NWC', 'WIO', 'NWC'), feature_group_count=xh.shape[-1])


def pool_mixer(xh, start_pos, w_pool, scale):
    B, T, C = xh.shape
    L = T - POOL_HIST
    xf = xh.astype(jnp.float32)
    cs = jnp.concatenate([jnp.zeros((B, 1, C), jnp.float32), jnp.cumsum(xf, axis=1)], axis=1)
    pos = start_pos + jnp.arange(L)
    outs = []
    for g, w in enumerate(POOL_WINDOWS):
        sl = slice(g * POOL_GC, (g + 1) * POOL_GC)
        s = cs[:, POOL_HIST + 1:POOL_HIST + 1 + L, sl] - cs[:, POOL_HIST + 1 - w:POOL_HIST + 1 - w + L, sl]
        cnt = jnp.minimum(w, pos + 1).astype(jnp.float32)
        outs.append(s / cnt[None, :, None] - xf[:, POOL_HIST:, sl])
    p = jnp.stack(outs, axis=2)
    y = jnp.einsum('blgc,gcd->blgd', p, w_pool.astype(jnp.float32)).reshape(B, L, C)
    y = y * scale.astype(jnp.float32)
    return y.astype(xh.dtype), xh[:, -POOL_HIST:]


def conformer_conv(a, gate, hist, w_dw, b_dw, ln_g, ln_b):
    z = a * jax.nn.sigmoid(gate)
    zh = jnp.concatenate([hist.astype(z.dtype), z], axis=1)
    c = (causal_depthwise(zh, w_dw) + b_dw).astype(jnp.float32)
    mu = jnp.mean(c, axis=-1, keepdims=True)
    var = jnp.mean(jnp.square(c - mu), axis=-1, keepdims=True)
    n = (c - mu) * lax.rsqrt(var + EPS) * ln_g.astype(jnp.float32) + ln_b.astype(jnp.float32)
    return jax.nn.silu(n).astype(a.dtype), zh[:, -(CONF_K - 1):]


def short_conv(xs, bg, cg, hist, w):
    z = cg * xs
    zh = jnp.concatenate([hist.astype(z.dtype), z], axis=1)
    return bg * causal_depthwise(zh, w), zh[:, -(SCONV_K - 1):]


def chunk_mlp(u, v, w_s, b_s):
    B, L, C = v.shape
    Lc = min(L, MLP_CHUNK)
    n = L // Lc
    ws = w_s[:, :Lc, :Lc] * jnp.tril(jnp.ones((Lc, Lc), w_s.dtype))
    vr = v.reshape(B, n, Lc, MLP_HEADS, MLP_HD)
    mixed = jnp.einsum('hij,bnjhd->bnihd', ws, vr) + b_s[:, :Lc].T[None, None, :, :, None]
    return u * mixed.reshape(B, L, C).astype(u.dtype)


def run_trunk(x, start_pos, pool_h, conv_h, sc_h, g_mix, w_in, w_pool, pool_scale,
              w_conf_dw, b_conf_dw, conf_ln_g, conf_ln_b, w_sconv, w_s, b_s, w_out,
              g_ffn, w_gate, w_up, w_down, g_final):
    B = x.shape[0]
    new_pool, new_conv, new_sc, new_v = [], [], [], []
    for l in range(DEPTH):
        if pool_h is None:
            ph = jnp.zeros((B, POOL_HIST, W_GROUP), x.dtype)
            ch = jnp.zeros((B, CONF_K - 1, W_GROUP), x.dtype)
            sh = jnp.zeros((B, SCONV_K - 1, W_GROUP), x.dtype)
        else:
            ph, ch, sh = pool_h[l].astype(x.dtype), conv_h[l], sc_h[l]
        h = rms_norm(x, g_mix[l])
        p = jnp.einsum('bld,de->ble', h, w_in[l])
        seg = [p[..., i * W_GROUP:(i + 1) * W_GROUP] for i in range(8)]
        ya, ph_new = pool_mixer(jnp.concatenate([ph, seg[0]], axis=1), start_pos, w_pool[l], pool_scale[l])
        yb, ch_new = conformer_conv(seg[1], seg[2], ch, w_conf_dw[l], b_conf_dw[l], conf_ln_g[l], conf_ln_b[l])
        yc, sh_new = short_conv(seg[3], seg[4], seg[5], sh, w_sconv[l])
        yd = chunk_mlp(seg[6], seg[7], w_s[l], b_s[l])
        mix = jnp.concatenate([ya, yb, yc, yd], axis=-1)
        x = x + jnp.einsum('ble,ed->bld', mix, w_out[l])
        f = rms_norm(x, g_ffn[l])
        hid = jax.nn.silu(jnp.einsum('bld,df->blf', f, w_gate[l])) * jnp.einsum('bld,df->blf', f, w_up[l])
        x = x + jnp.einsum('blf,fd->bld', hid, w_down[l])
        new_pool.append(ph_new)
        new_conv.append(ch_new)
        new_sc.append(sh_new)
        new_v.append(seg[7])
    y = rms_norm(x, g_final)
    return y, jnp.stack(new_pool), jnp.stack(new_conv), jnp.stack(new_sc), jnp.stack(new_v)


def setup_inputs(seed: int = 0) -> dict:
    key = jax.random.key(seed)
    ks = jax.random.split(key, 24)
    nrm = lambda k, s, sc: jax.random.normal(k, s, jnp.float32) * sc
    return {
        "x_prompt": nrm(ks[0], (BATCH, SEQ, D_MODEL), 1.0),
        "x_sample": nrm(ks[1], (DEC_BATCH, DEC_SEQ, D_MODEL), 1.0),
        "state_pool": nrm(ks[2], (DEPTH, DEC_BATCH, POOL_HIST, W_GROUP), 1.0),
        "state_conv": nrm(ks[3], (DEPTH, DEC_BATCH, CONF_K - 1, W_GROUP), 0.5),
        "state_short_conv": nrm(ks[4], (DEPTH, DEC_BATCH, SCONV_K - 1, W_GROUP), 0.5),
        "g_mix": 1.0 + nrm(ks[5], (DEPTH, D_MODEL), 0.1),
        "w_in": nrm(ks[6], (DEPTH, D_MODEL, D_IN), D_MODEL ** -0.5),
        "w_pool": nrm(ks[7], (DEPTH, N_POOL_GROUPS, POOL_GC, POOL_GC), POOL_GC ** -0.5),
        "pool_scale": 1.0 + nrm(ks[8], (DEPTH, W_GROUP), 0.1),
        "w_conf_dw": nrm(ks[9], (DEPTH, CONF_K, W_GROUP), CONF_K ** -0.5),
        "b_conf_dw": nrm(ks[10], (DEPTH, W_GROUP), 0.02),
        "conf_ln_g": 1.0 + nrm(ks[11], (DEPTH, W_GROUP), 0.1),
        "conf_ln_b": nrm(ks[12], (DEPTH, W_GROUP), 0.02),
        "w_sconv": nrm(ks[13], (DEPTH, SCONV_K, W_GROUP), SCONV_K ** -0.5),
        "w_s": nrm(ks[14], (DEPTH, MLP_HEADS, MLP_CHUNK, MLP_CHUNK), MLP_CHUNK ** -0.5),
        "b_s": 1.0 + nrm(ks[15], (DEPTH, MLP_HEADS, MLP_CHUNK), 0.1),
        "w_out": nrm(ks[16], (DEPTH, D_MIX, D_MODEL), D_MIX ** -0.5),
        "g_ffn": 1.0 + nrm(ks[17], (DEPTH, D_MODEL), 0.1),
        "w_gate": nrm(ks[18], (DEPTH, D_MODEL, D_FF), D_MODEL ** -0.5),
        "w_up": nrm(ks[19], (DEPTH, D_MODEL, D_FF), D_MODEL ** -0.5),
        "w_down": nrm(ks[20], (DEPTH, D_FF, D_MODEL), D_FF ** -0.5),
        "g_final": 1.0 + nrm(ks[21], (D_MODEL,), 0.1),
    }


def reference(x_prompt, x_sample, state_pool, state_conv, state_short_conv, g_mix, w_in,
              w_pool, pool_scale, w_conf_dw, b_conf_dw, conf_ln_g, conf_ln_b, w_sconv,
              w_s, b_s, w_out, g_ffn, w_gate, w_up, w_down, g_final):
    assert x_sample.shape[1] <= CHUNK
    weights = (g_mix, w_in, w_pool, pool_scale, w_conf_dw, b_conf_dw, conf_ln_g, conf_ln_b,
               w_sconv, w_s, b_s, w_out, g_ffn, w_gate, w_up, w_down, g_final)
    y_prompt, pool_p, conv_p, sc_p, _ = run_trunk(x_prompt, 0, None, None, None, *weights)
    y_sample, pool_s, conv_s, sc_s, v_s = run_trunk(
        x_sample, PAST_LEN, state_pool, state_conv, state_short_conv, *weights)
    return (y_prompt, y_sample, pool_p, pool_s, conv_p, conv_s, sc_p, sc_s, v_s)
```

```python
import contextlib
import numpy as np
import concourse.bass as bass
import concourse.mybir as mybir
from concourse.bass_utils import run_bass_kernel_spmd

F32 = mybir.dt.float32
BF16 = mybir.dt.bfloat16
AF = mybir.ActivationFunctionType
ALU = mybir.AluOpType

P = 128
D = 1024
KD = 8
DIN = 2048
DFF = 2816
KF = 22
WG = 256
DEPTH = 2
EPS = 1e-6
TT = 512
NS = 32
NT = TT + NS
NSLOT = 4
SLOT_ELEMS = 4096
ENGS = ("pe", "act", "dve", "pool", "sp")

NCP = 38


class Ins:
    __slots__ = ("eng", "fn", "deps", "signal", "rank", "idx", "is_dma", "semkey", "semval", "waitonly")

    def __init__(self, eng, fn):
        self.eng = eng
        self.fn = fn
        self.deps = ()
        self.signal = False
        self.rank = 0
        self.idx = 0
        self.is_dma = False
        self.semkey = None
        self.semval = 0
        self.waitonly = False


class Prog:
    def __init__(self, nc):
        self.nc = nc
        self.streams = {e: [] for e in ENGS}
        self.lastw = {}
        self.readers = {}
        self.dma_keys = {}
        self.pending_fence = {e: [] for e in ENGS}
        self.out_dmas = []

    def op(self, eng, fn, reads=(), writes=(), dma_key=None, extra=(), waitonly=False, nofence=False):
        ins = Ins(eng, fn)
        ins.waitonly = waitonly
        deps = set(extra)
        for r in reads:
            w = self.lastw.get(r)
            if w is not None:
                deps.add(w)
        for t in writes:
            w = self.lastw.get(t)
            if w is not None:
                deps.add(w)
            rd = self.readers.get(t)
            if rd:
                deps.update(rd.values())
        if self.pending_fence[eng] and not nofence:
            deps.update(self.pending_fence[eng])
            self.pending_fence[eng] = []
        if dma_key is not None:
            k = self.dma_keys.setdefault(dma_key, [0, None])
            if k[1] is not None:
                deps.add(k[1])
            k[0] += 16
            ins.is_dma = True
            ins.semkey = dma_key
            ins.semval = k[0]
            k[1] = ins
        deps.discard(ins)
        ins.deps = deps
        st = self.streams[eng]
        ins.idx = len(st)
        st.append(ins)
        for r in reads:
            d = self.readers.setdefault(r, {})
            d[("dma", id(ins)) if ins.is_dma else eng] = ins
        for t in writes:
            self.lastw[t] = ins
            self.readers[t] = {}
        return ins

    def fence(self, inss):
        for e in ENGS:
            self.pending_fence[e].extend(inss)

    def barrier(self):
        lasts = []
        for e in ENGS:
            for ins in reversed(self.streams[e]):
                if not ins.is_dma:
                    lasts.append(ins)
                    break
        for name, k in self.dma_keys.items():
            if k[1] is not None and not str(name).startswith(("ring", "xload", "wback")):
                lasts.append(k[1])
        self.fence(lasts)

    @staticmethod
    def _needs_wait(ins, d):
        if d.is_dma:
            return True
        if d.eng == ins.eng:
            return ins.eng != "pe"
        return True

    def emit(self):
        nc = self.nc
        for e in ENGS:
            for ins in self.streams[e]:
                for d in ins.deps:
                    if not d.is_dma and self._needs_wait(ins, d):
                        d.signal = True
        for e in ENGS:
            r = 0
            for ins in self.streams[e]:
                if ins.signal and not ins.is_dma:
                    r += 1
                ins.rank = r
        with contextlib.ExitStack() as es:
            esem = {e: es.enter_context(nc.semaphore("prog_" + e)) for e in ENGS}
            dsem = {k: es.enter_context(nc.semaphore("dma_%d" % i)) for i, k in enumerate(self.dma_keys)}
            block = es.enter_context(nc.Block())

            def run_stream(ename, e):
                waited = {}
                for ins in self.streams[ename]:
                    need = {}
                    for d in ins.deps:
                        if not self._needs_wait(ins, d):
                            continue
                        if d.is_dma:
                            key = ("d", d.semkey)
                            val = d.semval
                        else:
                            key = ("e", d.eng)
                            val = d.rank
                        if val > need.get(key, 0):
                            need[key] = val
                    todo = []
                    for key, val in need.items():
                        if waited.get(key, 0) >= val:
                            continue
                        waited[key] = val
                        todo.append((dsem[key[1]] if key[0] == "d" else esem[key[1]], val))
                    if ins.waitonly:
                        for sem, val in todo:
                            e.wait_ge(sem, val)
                        continue
                    if ins.is_dma:
                        for sem, val in todo:
                            e.wait_ge(sem, val)
                        todo = []
                    for sem, val in todo[1:]:
                        e.wait_ge(sem, val)
                    bi = ins.fn(e)
                    if todo:
                        bi._wait_ge(todo[0][0], todo[0][1])
                    if ins.is_dma:
                        bi.then_inc(dsem[ins.semkey], 16)
                    elif ins.signal:
                        bi.then_inc(esem[ename], 1)

            @block.tensor
            def _(e):
                run_stream("pe", e)

            @block.scalar
            def _(e):
                run_stream("act", e)

            @block.vector
            def _(e):
                run_stream("dve", e)

            @block.gpsimd
            def _(e):
                run_stream("pool", e)

            @block.sync
            def _(e):
                run_stream("sp", e)


class SubTile:
    def __init__(self, name, n, c0, sample, seq=0, ti=0, first=False, last=False):
        self.name = name
        self.n = n
        self.c0 = c0
        self.sample = sample
        self.seq = seq
        self.ti = ti
        self.first = first
        self.last = last
        self.cols = slice(c0, c0 + n)
        self.blocks = [(q, min(P, n - q * P)) for q in range((n + P - 1) // P)]


DEBUG = False


def build_program(n_seq, seq_len, with_sample=True):
    nc = bass.Bass("TRN2", target_bir_lowering=False)
    pg = Prog(nc)
    ntile = seq_len // TT
    assert seq_len % TT == 0

    def din(name, shape, dt=F32):
        return nc.dram_tensor(name, list(shape), dt, kind="ExternalInput").ap()

    def dout(name, shape, dt=F32):
        return nc.dram_tensor(name, list(shape), dt, kind="ExternalOutput").ap()

    x_prompt = din("x_prompt", [n_seq, seq_len, D])
    x_sample = din("x_sample", [NS, D])
    st_pack = din("st_pack", [32, DEPTH, 3, WG])
    g_final = din("g_final", [1, D])
    w_in = din("w_in", [DEPTH, D, DIN])
    w_s = din("w_s", [DEPTH, 4, P, P])
    w_out = din("w_out", [DEPTH, D, D])
    w_gate = din("w_gate", [DEPTH, D, DFF])
    w_up = din("w_up", [DEPTH, D, DFF])
    w_down = din("w_down", [DEPTH, DFF, D])
    c_pack = din("c_pack", [P, 290])
    c_hsel = din("c_hsel", [8, 2, P + 2])
    pm_pack = din("pm_pack", [NCP, DEPTH, WG])
    gm_pack = din("gm_pack", [4 * KD, P])
    bs_pack = din("bs_pack", [8, DEPTH, P])
    pw_pack = din("pw_pack", [P, DEPTH, 2, P])

    y_prompt = dout("y_prompt", [n_seq, seq_len, D])
    y_sample = dout("y_sample", [NS, D])
    o_pool_p = dout("o_pool_p", [DEPTH, n_seq, 15, WG])
    o_pool_s = dout("o_pool_s", [DEPTH, 15, WG])
    o_conv_p = dout("o_conv_p", [DEPTH, n_seq, 30, WG])
    o_conv_s = dout("o_conv_s", [DEPTH, 30, WG])
    o_sc_p = dout("o_sc_p", [DEPTH, n_seq, 2, WG])
    o_sc_s = dout("o_sc_s", [DEPTH, 2, WG])
    o_v_s = dout("o_v_s", [DEPTH, NS, WG])

    if DEBUG:
        dbg_mix = nc.dram_tensor("dbg_mix", [P, KD, TT], BF16, kind="ExternalOutput").ap()
        dbg_x1 = dout("dbg_x1", [P, KD, TT])
        dbg_x2 = dout("dbg_x2", [P, KD, TT])
        dbg_hn = nc.dram_tensor("dbg_hn", [P, KD, TT], BF16, kind="ExternalOutput").ap()

    def dscr(name, shape):
        return nc.dram_tensor(name, list(shape), BF16, kind="Internal").ap()

    WINb = [dscr("WINb%d" % l, [D, DIN]) for l in range(DEPTH)]
    WOUTb = [dscr("WOUTb%d" % l, [D, D]) for l in range(DEPTH)]
    WGb = [dscr("WGb%d" % l, [D, DFF]) for l in range(DEPTH)]
    WUb = [dscr("WUb%d" % l, [D, DFF]) for l in range(DEPTH)]
    WDb = [dscr("WDb%d" % l, [DFF, D]) for l in range(DEPTH)]

    def sb(name, shape, dt=F32):
        return nc.alloc_sbuf_tensor(name, list(shape), dt)

    xs_m = sb("xs_m", [P, 4, D])
    x = sb("x", [P, KD, NT])
    hn = sb("hn", [P, KD, NT], BF16)
    mix = sb("mix", [P, KD, NT], BF16)
    rstd = sb("rstd", [P, NT])
    lnt = sb("lnt", [P, NT])
    sqt = [sb("sqt%d" % i, [P, NT], BF16) for i in range(2)]
    PBm = [sb("PBm%d" % l, [P, 2, 16 + TT]) for l in range(DEPTH)]
    PBs = [sb("PBs%d" % l, [P, 2, 16 + NS]) for l in range(DEPTH)]
    ZBm = [sb("ZBm%d" % l, [P, 2, 32 + TT]) for l in range(DEPTH)]
    ZBs = [sb("ZBs%d" % l, [P, 2, 32 + NS]) for l in range(DEPTH)]
    CBm = [sb("CBm%d" % l, [P, 2, 2 + TT]) for l in range(DEPTH)]
    CBs = [sb("CBs%d" % l, [P, 2, 2 + NS]) for l in range(DEPTH)]
    ring = [sb("ring%d" % i, [P, SLOT_ELEMS], BF16) for i in range(NSLOT)]
    cpk = sb("cpk", [P, 290])
    ident = cpk[:, 0:128]
    tril = cpk[:, 128:256]
    ones_m = sb("ones_m", [P, P], BF16)
    ones_c = sb("ones_c", [P, P], BF16)
    wsT = sb("wsT", [P, DEPTH, 4, P], BF16)
    poolW = sb("poolW", [P, DEPTH, 2, P], BF16)
    CP = sb("CP", [P, DEPTH, 2, NCP])
    G = sb("G", [P, 4 * KD])
    gfin = sb("gfin", [P, D])
    rc16 = cpk[:, 256:288].rearrange("p (j t) -> p j t", j=2)
    invw = cpk[:, 288:290]
    hsel = sb("hsel", [8, 2, P], BF16)
    so = [sb("so%d" % i, [32, WG]) for i in range(4)]
    ss = sb("ss", [P, 16])
    Wdg = sb("Wdg", [P, 62, P], BF16)
    ZBbm = [sb("ZBbm%d" % l, [P, 2, 32 + TT], BF16) for l in range(DEPTH)]
    ZBbs = [sb("ZBbs%d" % l, [P, 2, 32 + NS], BF16) for l in range(DEPTH)]
    identb = sb("identb", [P, P], BF16)
    dmy = sb("dmy", [P, 4])
    rs1 = sb("rs1", [P, 4])
    xs_s = sb("xs_s", [NS, D])

    ARENA = 54400 - 4352
    arena = sb("arena", [P, ARENA // 2], BF16)
    a_off = [0]

    def aview(nbytes_shape, dt, at=None):
        shape = list(nbytes_shape)
        n = int(np.prod(shape))
        esz = 4 if dt == F32 else 2
        nbytes = n * esz
        off = a_off[0] if at is None else at
        off = (off + 31) // 32 * 32
        if at is None:
            a_off[0] = off + nbytes
        assert off + nbytes <= ARENA, (off, nbytes)
        v = arena[:, off // 2:(off + nbytes) // 2]
        if dt == F32:
            v = v.bitcast(F32)
        if len(shape) == 2:
            v = v.rearrange("p (a b) -> p a b", a=shape[0])
        elif len(shape) == 3:
            v = v.rearrange("p (a b c) -> p a b c", a=shape[0], b=shape[1])
        return v

    S2m = aview([2, 16 + TT], F32)
    S4m = aview([2, 16 + TT], F32)
    S8m = aview([1, 16 + TT], F32)
    S16m = aview([1, 16 + TT], F32)
    S2s = aview([2, 16 + NS], F32)
    S4s = aview([2, 16 + NS], F32)
    S8s = aview([1, 16 + NS], F32)
    S16s = aview([1, 16 + NS], F32)
    Dp = aview([2, NT], BF16)
    sig = aview([2, NT], F32)
    cacc = aview([2, 1, NT], F32)
    cbf = aview([2, NT], BF16)
    sqd = aview([2, NT], BF16)
    xsc = aview([2, NT], F32)
    Bsb = aview([2, NT], F32)
    acc3 = aview([2, NT], F32)
    ub = aview([2, NT], F32)
    vtok = aview([5, WG], BF16)
    pt16 = aview([2, 16], F32)
    hid = aview([KF, NT], BF16, at=0)
    sg = [aview([1, NT], F32, at=KF * NT * 2 + 64 + i * (NT * 4 + 32)) for i in range(2)]
    ys_at = KF * NT * 2 + 64 + 2 * (NT * 4 + 32) + 64
    ys = aview([4, D], F32, at=ys_at)
    wsf = aview([DEPTH * 4, P], F32, at=0)
    stg = aview([DEPTH, 3, WG], F32, at=4096)
    poolWf = aview([DEPTH, 2, P], F32, at=4096 + 6144)
    PM = aview([DEPTH, WG], F32, at=4096 + 6144 + 2048)
    hself_a = aview([2, P + 2], F32, at=20480)
    bsf_a = aview([DEPTH, P], F32, at=4096 + 6144 + 2048 + 2048 + 1024)
    bback_a = aview([DEPTH, P], F32, at=4096 + 6144 + 2048 + 2048 + 2048)
    hself = hself_a[0:8, :, :]
    GM_a = aview([1, P], F32, at=4096 + 6144 + 2048 + 2048 + 3072)
    GM = GM_a[0:4 * KD, 0, :]
    bhi = aview([DEPTH, P], BF16, at=4096 + 6144 + 2048 + 2048 + 3072 + 512)[0:8, :, :]
    blo = aview([DEPTH, P], BF16, at=4096 + 6144 + 2048 + 2048 + 3072 + 1024)[0:8, :, :]
    bsf = bsf_a[0:8, :, :]
    bback = bback_a[0:8, :, :]
    bhl4 = sb("bhl4", [8, DEPTH, 4 * P], BF16)

    ps = [nc.alloc_psum_tensor("ps%d" % i, [P, 512], F32) for i in range(8)]
    bank_ctr = {"mm": 0, "st": 0, "aux": 0}
    bank_sets = {"mm": [0, 1, 2, 3], "st": [4, 5], "aux": [6, 7]}

    def getbank(cls):
        s = bank_sets[cls]
        b = s[bank_ctr[cls] % len(s)]
        bank_ctr[cls] += 1
        return b

    def pst(b):
        return ("ps", b)

    misc_ctr = [0]

    def misc_key():
        misc_ctr[0] += 1
        return "misc%d" % (misc_ctr[0] % 64)

    NOFENCE = [False]

    def dma(eng, out, in_, reads=(), writes=(), key=None, is_out=False):
        ins = pg.op(eng, lambda e, o=out, i=in_: e.dma_start(out=o, in_=i), reads=reads, writes=writes,
                    dma_key=key or misc_key(), nofence=NOFENCE[0])
        if is_out:
            pg.out_dmas.append(ins)
        return ins

    def mm(out, lhsT, rhs, start, stop, reads, writes, tile_position=None):
        if tile_position is None:
            fn = lambda e: e.matmul(out, lhsT, rhs, start=start, stop=stop)
        else:
            fn = lambda e: e.matmul(out, lhsT, rhs, start=start, stop=stop, tile_position=tile_position)
        return pg.op("pe", fn, reads=reads, writes=writes)

    def tr(out, in_, idn, reads, writes):
        return pg.op("pe", lambda e: e.transpose(out, in_, idn), reads=reads, writes=writes)

    def act(out, in_, func, reads, writes, scale=None, bias=None, accum_out=None):
        kw = {}
        if scale is not None:
            kw["scale"] = scale
        if bias is not None:
            kw["bias"] = bias
        if accum_out is not None:
            kw["accum_out"] = accum_out
        return pg.op("act", lambda e: e.activation(out=out, in_=in_, func=func, **kw), reads=reads, writes=writes)

    def tt(eng, out, in0, in1, op, reads, writes):
        return pg.op(eng, lambda e: e.tensor_tensor(out=out, in0=in0, in1=in1, op=op), reads=reads, writes=writes)

    def tsc(eng, out, in0, s1, s2, op0, op1, reads, writes):
        if s2 is None:
            fn = lambda e: e.tensor_scalar(out=out, in0=in0, scalar1=s1, scalar2=None, op0=op0)
        else:
            fn = lambda e: e.tensor_scalar(out=out, in0=in0, scalar1=s1, scalar2=s2, op0=op0, op1=op1)
        return pg.op(eng, fn, reads=reads, writes=writes)

    def stt(eng, out, in0, scalar, in1, op0, op1, reads, writes):
        return pg.op(eng, lambda e: e.scalar_tensor_tensor(out=out, in0=in0, scalar=scalar, in1=in1, op0=op0, op1=op1),
                     reads=reads, writes=writes)

    def cp(eng, out, in_, reads, writes):
        if eng == "act":
            return pg.op("act", lambda e: e.copy(out=out, in_=in_), reads=reads, writes=writes)
        return pg.op(eng, lambda e: e.tensor_copy(out=out, in_=in_), reads=reads, writes=writes)

    def memset(eng, ap, val, writes):
        return pg.op(eng, lambda e: e.memset(ap, val), writes=writes)

    dma("sp", cpk[:], c_pack, writes=[("c", 0)])
    dma("act", GM[:, :], gm_pack, writes=[("c", 1)])
    dma("sp", PM[0:NCP, :, :], pm_pack, writes=[("c", 2)])
    dma("act", hself[:], c_hsel, writes=[("c", 3)])
    dma("sp", wsf[:, :, :], w_s.rearrange("l h i j -> i (l h) j"), writes=[("c", 4)])
    dma("act", bsf[:, :, :], bs_pack, writes=[("c", 5)])
    dma("sp", poolWf[:, :, :, :], pw_pack, writes=["pwf"])
    dma("act", gfin[:], g_final.broadcast_to([P, D]), writes=[("c", 6)])
    if with_sample:
        dma("sp", stg[0:32, :, :, :], st_pack, writes=[("c", 7)])
    memset("dve", ones_m[:], 1.0 / 1024.0, writes=["c"])
    memset("dve", ones_c[:], 1.0 / 256.0, writes=["c"])
    pg.barrier()
    cp("dve", poolW[:], poolWf[:], reads=["pwf"], writes=[("c2", 1)])
    cp("dve", hsel[:], hself[:, :, 0:P], reads=[], writes=[("c2", 2)])
    cp("dve", bhi[:], bsf[:], reads=[], writes=["bhi"])
    cp("dve", bback[:], bhi[:], reads=["bhi"], writes=["bback"])
    tt("dve", bback[:], bsf[:], bback[:], ALU.subtract, reads=["bback"], writes=["bback"])
    cp("dve", blo[:], bback[:], reads=["bback"], writes=[("c2", 3)])
    tsc("dve", bback[:], bback[:], hself[:, 0, P + 1:P + 2], None, ALU.mult, None, reads=["bback"], writes=["bback"])
    stt("dve", bback[:], bhi[:], hself[:, 0, P:P + 1], bback[:], ALU.mult, ALU.add, reads=["bback", "bhi"],
        writes=["bback"])
    for r_ in range(4):
        cp("dve", bhl4[:, :, r_ * P:(r_ + 1) * P], bback[:, :, :], reads=["bback"], writes=[("c2", 4)])
    for l in range(DEPTH):
        for h in range(4):
            tt("dve", wsf[:, l * 4 + h, :], wsf[:, l * 4 + h, :], tril[:], ALU.mult, reads=[("wsf", l, h)],
               writes=[("wsf", l, h)])
    for l in range(DEPTH):
        b = getbank("mm")
        for h in range(4):
            tr(ps[b][:, h * P:(h + 1) * P], wsf[:, l * 4 + h, :], ident[:], reads=[("wsf", l, h)], writes=[pst(b)])
        cp("act", wsT[:, l, :, :], ps[b][:, :].rearrange("p (h i) -> p h i", h=4), reads=[pst(b)], writes=[pst(b), ("c2", 5)])
    for l in range(DEPTH):
        b = getbank("mm")
        for j in range(2):
            tr(ps[b][:, j * NCP:(j + 1) * NCP], PM[0:NCP, l, j * P:(j + 1) * P], ident[0:NCP, 0:NCP], reads=[],
               writes=[pst(b)])
        cp("act", CP[:, l, :, :], ps[b][:, 0:2 * NCP].rearrange("p (j c) -> p j c", j=2), reads=[pst(b)],
           writes=[pst(b), ("c2", 6)])
    b = getbank("mm")
    tr(ps[b][:, 0:4 * KD], GM[:, :], ident[0:4 * KD, 0:4 * KD], reads=[], writes=[pst(b)])
    cp("act", G[:], ps[b][:, 0:4 * KD], reads=[pst(b)], writes=[pst(b), ("c2", 7)])
    if with_sample:
        for l in range(DEPTH):
            memset("dve", PBs[l][:, :, 0:1], 0.0, writes=[("PB", l, "s")])
            b = getbank("mm")
            for (kind, nr, c0) in ((0, 15, 0), (1, 30, 64), (2, 2, 128)):
                for j in range(2):
                    tr(ps[b][:, c0 + j * 32:c0 + j * 32 + nr], stg[0:nr, l, kind, j * P:(j + 1) * P],
                       ident[0:nr, 0:nr], reads=[], writes=[pst(b)])
            for j in range(2):
                cp("dve", PBs[l][:, j, 1:16], ps[b][:, j * 32:j * 32 + 15], reads=[pst(b)], writes=[("PB", l, "s")])
                cp("dve", ZBs[l][:, j, 2:32], ps[b][:, 64 + j * 32:64 + j * 32 + 30], reads=[pst(b)],
                   writes=[("ZB", l, "s")])
                cp("act", CBs[l][:, j, 0:2], ps[b][:, 128 + j * 32:128 + j * 32 + 2], reads=[pst(b)],
                   writes=[pst(b), ("CB", l, "s")])
    pg.barrier()

    def layer_stream(l):
        ent = []

        def add(kind, g, fsrc, bsrc, shape):
            ra = lambda ap: ap.rearrange("(k p) e -> p k e", p=P)
            ent.append((kind, l, g, ra(bsrc), shape, ("wbk", kind, l, g), ra(fsrc)))
        for g in (1, 0, 2, 3):
            add("win", g, w_in[l][:, g * 512:(g + 1) * 512], WINb[l][:, g * 512:(g + 1) * 512], (KD, 512))
        for g in range(2):
            add("wout", g, w_out[l][:, g * 512:(g + 1) * 512], WOUTb[l][:, g * 512:(g + 1) * 512], (KD, 512))
        for g in range(6):
            w = 512 if g < 5 else 256
            add("wg", g, w_gate[l][:, g * 512:g * 512 + w], WGb[l][:, g * 512:g * 512 + w], (KD, w))
            add("wu", g, w_up[l][:, g * 512:g * 512 + w], WUb[l][:, g * 512:g * 512 + w], (KD, w))
        for g in range(4):
            for h in range(2):
                add("wd", g * 2 + h, w_down[l][h * 1408:(h + 1) * 1408, g * 256:(g + 1) * 256],
                    WDb[l][h * 1408:(h + 1) * 1408, g * 256:(g + 1) * 256], (11, 256))
        return ent

    n_pass = n_seq * ntile
    stream = []
    for _ in range(n_pass):
        for l in range(DEPTH):
            stream.extend(layer_stream(l))
    per_pass = len(stream) // n_pass
    st_issued = [0]
    st_used = [0]

    def issue_upto(n):
        while st_issued[0] < min(n, len(stream)):
            i = st_issued[0]
            kind, l, g, bsrc, (kk, w), tok, fsrc = stream[i]
            slot = i % NSLOT
            view = ring[slot][:, 0:kk * w].rearrange("p (k e) -> p k e", k=kk)
            if i < per_pass:
                dma("pool", view, fsrc, writes=[("ring", slot)], key="ringc%d" % slot)
                if n_pass > 1:
                    dma("sp", bsrc, view, reads=[("ring", slot)], writes=[tok], key="wback%d" % slot)
            else:
                dma("sp", view, bsrc, reads=[tok], writes=[("ring", slot)], key="ring%d" % slot)
            st_issued[0] += 1

    def next_w(kind, l, g):
        i = st_used[0]
        e = stream[i]
        assert (e[0], e[1], e[2]) == (kind, l, g), (e[:3], kind, l, g)
        assert i < st_issued[0]
        st_used[0] += 1
        slot = i % NSLOT
        kk, w = e[4]
        return ring[slot][:, 0:kk * w].rearrange("p (k e) -> p k e", k=kk), ("ring", slot)

    def w_advance():
        issue_upto(st_used[0] + NSLOT)

    bgq = []
    pp2_pending = [0]

    def drain(n):
        for _ in range(min(n, len(bgq))):
            bgq.pop(0)()

    def drain_all():
        drain(len(bgq))

    nstate = {}

    def norm_sq(st, k):
        n, cols, sn = st.n, st.cols, st.name
        d = nstate.setdefault(sn, {"bank": None, "cnt": 0})
        if d["cnt"] == 0:
            d["bank"] = getbank("st")
            act(dmy[:, 0:1], epsb[:, 0:1], AF.Ln, reads=[], writes=["dmy"])
        i = d["cnt"]
        d["cnt"] += 1
        r = sq_ctr[0] % 2
        sq_ctr[0] += 1
        s = sqt[r]
        act(s[:, 0:n], x[:, k, cols], AF.Square, reads=[("x", k, sn)], writes=[("sqt", r)])
        b = d["bank"]

        def thunk():
            mm(ps[b][:, 0:n], ones_m[:], s[:, 0:n], i == 0, i == KD - 1, reads=[("sqt", r)], writes=[pst(b)])
        return thunk

    def norm_finish(st, gcol):
        n, cols, sn = st.n, st.cols, st.name
        d = nstate.pop(sn)
        assert d["cnt"] == KD
        b = d["bank"]
        act(lnt[:, cols], ps[b][:, 0:n], AF.Ln, reads=[pst(b)], writes=[pst(b), ("lnt", sn)], bias=EPS_AP[0])
        act(rstd[:, cols], lnt[:, cols], AF.Exp, reads=[("lnt", sn)], writes=[("rstd", sn)], scale=-0.5)
        for k in range(KD):
            stt("dve", hn[:, k, cols], x[:, k, cols], G[:, gcol + k:gcol + k + 1], rstd[:, cols], ALU.mult, ALU.mult,
                reads=[("x", k, sn), ("rstd", sn)], writes=[("hn", k, sn)])

    def flush_keep1(pend):
        while len(pend) > 1:
            pend.pop(0)()

    def flush(pend):
        while pend:
            pend.pop(0)()

    def hist_bufs(l, st):
        if st.sample:
            return PBs[l], ZBs[l], CBs[l], S2s, S4s, S8s, S16s
        return PBm[l], ZBm[l], CBm[l], S2m, S4m, S8m, S16m

    def pool_part1(l, st):
        n, cols, sn = st.n, st.cols, st.name
        PB, ZB, CB, S2, S4, S8, S16 = hist_bufs(l, st)
        W = 16 + n
        tPB = ("PB", l, sn)
        e_ = "pool"
        tt(e_, S2[:, :, 1:W], PB[:, :, 1:W], PB[:, :, 0:W - 1], ALU.add, reads=[tPB], writes=[("S2", sn)])
        tt(e_, S4[:, :, 3:W], S2[:, :, 3:W], S2[:, :, 1:W - 2], ALU.add, reads=[("S2", sn)], writes=[("S4", sn)])
        tt(e_, S8[:, 0, 7:W], S4[:, 1, 7:W], S4[:, 1, 3:W - 4], ALU.add, reads=[("S4", sn)], writes=[("S8", sn)])
        tt(e_, S16[64:128, 0, 15:W], S8[64:128, 0, 15:W], S8[64:128, 0, 7:W - 8], ALU.add, reads=[("S8", sn)],
           writes=[("S16", sn)])
        grp = ((0, 0, 64, S2[0:64, 0, :], ("S2", sn)), (0, 64, 128, S4[64:128, 0, :], ("S4", sn)),
               (1, 0, 64, S8[0:64, 0, :], ("S8", sn)), (1, 64, 128, S16[64:128, 0, :], ("S16", sn)))
        for gi, (j, p0, p1, sv, stok) in enumerate(grp):
            stt("dve", Dp[p0:p1, j, cols], sv[:, 16:W], invw[p0:p1, j:j + 1], PB[p0:p1, j, 16:W],
                ALU.mult, ALU.subtract, reads=[stok, tPB], writes=[("Dp", j, sn)])
        if st.first and not st.sample:
            for (j, p0, p1, sv, stok) in grp:
                tt("dve", pt16[p0:p1, j, :], sv[:, 16:32], rc16[p0:p1, j, :], ALU.mult, reads=[stok],
                   writes=[("pt16", j, p0)])
                tt("dve", Dp[p0:p1, j, st.c0:st.c0 + 16], pt16[p0:p1, j, :], PB[p0:p1, j, 16:32], ALU.subtract,
                   reads=[("pt16", j, p0), tPB], writes=[("Dp", j, sn)])

    def pool_part2(l, st):
        n, cols, sn = st.n, st.cols, st.name
        for j in range(2):
            b = getbank("mm")
            mm(ps[b][:, 0:n], poolW[:, l, j, :], Dp[:, j, cols], True, True, reads=[("Dp", j, sn)], writes=[pst(b)])
            tsc("dve", mix[:, j, cols], ps[b][:, 0:n], CP[:, l, j, 0:1], None, ALU.mult, None, reads=[pst(b)],
                writes=[pst(b), ("mix", j, sn)])

    def zbb_of(l, st):
        return ZBbs[l] if st.sample else ZBbm[l]

    def conv_thunks(l, st):
        n, cols, sn = st.n, st.cols, st.name
        ZBb = zbb_of(l, st)
        tzb = ("ZBb", l, sn)
        cpl = CP[:, l, :, :]
        out = []

        def tile(j):
            b = getbank("mm")
            for k in range(31):
                mm(ps[b][:, 0:n], Wdg[:, j * 31 + k, :], ZBb[:, j, 2 + k:2 + k + n], k == 0, k == 30,
                   reads=[tzb, ("Wdg", j * 31 + k)], writes=[pst(b)])
            act(cacc[:, j, 0, cols], ps[b][:, 0:n], AF.Identity, reads=[pst(b)], writes=[pst(b), ("cacc", j, 0, sn)],
                bias=cpl[:, j, 32:33])
        for j in range(2):
            out.append(lambda j=j: tile(j))
        return out

    def wdg_thunks(l):
        out = []
        for j in range(2):
            for k in range(31):
                out.append(lambda j=j, k=k: act(Wdg[:, j * 31 + k, :], identb[:], AF.Identity, reads=[],
                                                writes=[("Wdg", j * 31 + k)], scale=CP[:, l, j, 1 + k:2 + k]))
        return out

    WIN_ORDER = {1: (4, 5, 6, 7), 0: (0, 1, 2, 3), 2: (8, 9, 10, 11), 3: (12, 13)}

    def w_in_stage(l, sts, pre_thunks=()):
        npre = [len(pre_thunks)]
        bgq.extend(pre_thunks)

        def drain_step():
            if npre[0] > 0:
                k_ = min(6, npre[0])
                npre[0] -= k_
                drain(k_)
            else:
                drain(1)
        for g in (1, 0, 2, 3):
            slot, rtok = next_w("win", l, g)
            pre = {}
            if g == 1:
                st0 = sts[0]
                for j in WIN_ORDER[g]:
                    pre[j] = getbank("mm")
                for k in range(KD):
                    for j in WIN_ORDER[g]:
                        mm(ps[pre[j]][:, 0:st0.n], slot[:, k, (j % 4) * P:(j % 4 + 1) * P], hn[:, k, st0.cols], k == 0,
                           k == KD - 1, reads=[rtok, ("hn", k, st0.name)], writes=[pst(pre[j])])
            for j in WIN_ORDER[g]:
                jj = j % 4
                for st in sts:
                    n, cols, sn = st.n, st.cols, st.name
                    PB, ZB, CB = hist_bufs(l, st)[:3]
                    if st is sts[0] and j in pre:
                        b = pre[j]
                    else:
                        b = getbank("mm")
                        for k in range(KD):
                            mm(ps[b][:, 0:n], slot[:, k, jj * P:(jj + 1) * P], hn[:, k, cols], k == 0, k == KD - 1,
                               reads=[rtok, ("hn", k, sn)], writes=[pst(b)])
                    src = ps[b][:, 0:n]
                    if j in (0, 1):
                        cp("act", PB[:, j, 16:16 + n], src, reads=[pst(b)], writes=[pst(b), ("PB", l, sn)])
                    elif j in (2, 3):
                        tt("dve", ZB[:, j - 2, 32:32 + n], src, sig[:, j - 2, cols], ALU.mult,
                           reads=[pst(b), ("sig", j - 2, sn)], writes=[pst(b), ("ZB", l, sn)])
                        cp("act", zbb_of(l, st)[:, j - 2, 32:32 + n], ZB[:, j - 2, 32:32 + n], reads=[("ZB", l, sn)],
                           writes=[("ZBb", l, sn)])
                    elif j in (4, 5):
                        act(sig[:, j - 4, cols], src, AF.Sigmoid, reads=[pst(b)], writes=[pst(b), ("sig", j - 4, sn)])
                    elif j in (6, 7):
                        cp("act", xsc[:, j - 6, cols], src, reads=[pst(b)], writes=[pst(b), ("xsc", j - 6, sn)])
                    elif j in (8, 9):
                        cp("act", Bsb[:, j - 8, cols], src, reads=[pst(b)], writes=[pst(b), ("Bsb", j - 8, sn)])
                    elif j in (10, 11):
                        tt("dve", CB[:, j - 10, 2:2 + n], src, xsc[:, j - 10, cols], ALU.mult,
                           reads=[pst(b), ("xsc", j - 10, sn)], writes=[pst(b), ("CB", l, sn)])
                    elif j in (12, 13):
                        cp("act", ub[:, j - 12, cols], src, reads=[pst(b)], writes=[pst(b), ("u", j - 12, sn)])
                    drain_step()
                if j == 5:
                    act(dmy[:, 0:1], epsb[:, 0:1], AF.Ln, reads=[], writes=["dmy"])
                if j == 3:
                    if npre[0] > 0:
                        drain(npre[0])
                        npre[0] = 0
                    for st in sts:
                        pool_part1(l, st)
                    for st in sts:
                        th = conv_thunks(l, st)
                        bgq.append(th[0])
                        pp2_pending[0] += 1
                        bgq.append(lambda st=st: (pool_part2(l, st), pp2_pending.__setitem__(0, pp2_pending[0] - 1)))
                        bgq.append(th[1])
            if g == 3:
                for st in sts:
                    sn = st.name
                    bb = [getbank("aux"), getbank("aux")]
                    for (q, nq) in st.blocks:
                        b = bb[q // 2]
                        o = ps[b][0:nq, (q % 2) * WG:(q % 2 + 1) * WG]
                        for k in range(KD):
                            mm(o, hn[:, k, st.c0 + q * P:st.c0 + q * P + nq], slot[:, k, 256:512], k == 0, k == KD - 1,
                               reads=[rtok, ("hn", k, sn)], writes=[pst(b)])
                        drain_step()
                    if st.sample:
                        cp("act", vtok[0:NS, 4, :], ps[bb[0]][0:NS, 0:WG], reads=[pst(bb[0])], writes=[("vtok", sn)])
                        cp("dve", so[3][0:NS, :], ps[bb[0]][0:NS, 0:WG], reads=[pst(bb[0])],
                           writes=[pst(bb[0]), ("so", 3)])
                        dma("sp", o_v_s[l], so[3][0:NS, :], reads=[("so", 3)], key="so3", is_out=True)
                    else:
                        for hb in range(2):
                            cp("act", vtok[:, 2 * hb:2 * hb + 2, :], ps[bb[hb]][:, :].rearrange("p (q c) -> p q c", q=2),
                               reads=[pst(bb[hb])], writes=[pst(bb[hb]), ("vtok", sn)])
            w_advance()

    def state_out(l, st, kind, buf, nr, hist, tok, dst):
        n, sn = st.n, st.name
        W = hist + n
        b = getbank("mm")
        for j in range(2):
            tr(ps[b][0:nr, j * P:(j + 1) * P], buf[:, j, W - nr:W], ident[:], reads=[tok], writes=[pst(b)])
        cp("dve", so[kind][0:nr, :], ps[b][0:nr, 0:WG], reads=[pst(b)], writes=[pst(b), ("so", kind)])
        dma("sp", dst, so[kind][0:nr, :], reads=[("so", kind)], key="so%d" % kind, is_out=True)

    def mixers_a(l, sts):
        while pp2_pending[0] > 0:
            drain(1)
        cpl = CP[:, l, :, :]
        for st in sts:
            n, cols, sn = st.n, st.cols, st.name
            PB, ZB, CB, S2, S4, S8, S16 = hist_bufs(l, st)
            tCB = ("CB", l, sn)
            for k in range(3):
                for j in range(2):
                    if k == 0:
                        tsc("dve", acc3[:, j, cols], CB[:, j, 0:n], cpl[:, j, 35:36], None, ALU.mult, None, reads=[tCB],
                            writes=[("acc3", j, sn)])
                    else:
                        stt("dve", acc3[:, j, cols], CB[:, j, k:k + n], cpl[:, j, 35 + k:36 + k], acc3[:, j, cols],
                            ALU.mult, ALU.add, reads=[tCB, ("acc3", j, sn)], writes=[("acc3", j, sn)])
            for j in range(2):
                tt("dve", mix[:, 4 + j, cols], acc3[:, j, cols], Bsb[:, j, cols], ALU.mult,
                   reads=[("acc3", j, sn), ("Bsb", j, sn)], writes=[("mix", 4 + j, sn)])
            for j in range(2):
                bq = getbank("aux")
                nblk = len(st.blocks)
                for hh in range(2):
                    tp = (0, 64) if hh else None
                    h_ = 2 * j + hh
                    o_all = ps[bq][hh * 64:(hh + 1) * 64, 0:n]
                    mm(o_all, hsel[:, j, hh * 64:(hh + 1) * 64], bhl4[:, l, 0:n], True, False, reads=[("vtok", sn)],
                       writes=[pst(bq)], tile_position=tp)
                    for bi_, (q, nq) in enumerate(st.blocks):
                        qi = 4 if st.sample else q
                        o = ps[bq][hh * 64:(hh + 1) * 64, q * P:q * P + nq]
                        mm(o, vtok[0:nq, qi, h_ * 64:(h_ + 1) * 64], wsT[0:nq, l, h_, 0:nq], False, bi_ == nblk - 1,
                           reads=[("vtok", sn)], writes=[pst(bq)], tile_position=tp)
                tt("dve", mix[:, 6 + j, cols], ps[bq][:, 0:n], ub[:, j, cols], ALU.mult, reads=[pst(bq), ("u", j, sn)],
                   writes=[pst(bq), ("mix", 6 + j, sn)])

    def mixers_b(l, sts):
        drain_all()
        cpl = CP[:, l, :, :]
        for st in sts:
            n, cols, sn = st.n, st.cols, st.name
            b = getbank("st")
            for j in range(2):
                cp("act", cbf[:, j, cols], cacc[:, j, 0, cols], reads=[("cacc", j, 0, sn)], writes=[("cbf", j, sn)])
                mm(ps[b][:, 0:n], ones_c[:], cbf[:, j, cols], j == 0, j == 1, reads=[("cbf", j, sn)], writes=[pst(b)])
            b2 = getbank("st")
            for j in range(2):
                tt("dve", cacc[:, j, 0, cols], cacc[:, j, 0, cols], ps[b][:, 0:n], ALU.subtract,
                   reads=[("cacc", j, 0, sn), pst(b)], writes=[("cacc", j, 0, sn)] + ([pst(b)] if j == 1 else []))
                act(sqd[:, j, cols], cacc[:, j, 0, cols], AF.Square, reads=[("cacc", j, 0, sn)], writes=[("sqd", j, sn)])
                mm(ps[b2][:, 0:n], ones_c[:], sqd[:, j, cols], j == 0, j == 1, reads=[("sqd", j, sn)], writes=[pst(b2)])
            act(lnt[:, cols], ps[b2][:, 0:n], AF.Ln, reads=[pst(b2)], writes=[pst(b2), ("lnt", sn)], bias=EPS_AP[0])
            act(rstd[:, cols], lnt[:, cols], AF.Exp, reads=[("lnt", sn)], writes=[("rstd", sn)], scale=-0.5)
            for j in range(2):
                tt("dve", cacc[:, j, 0, cols], cacc[:, j, 0, cols], rstd[:, cols], ALU.mult,
                   reads=[("cacc", j, 0, sn), ("rstd", sn)], writes=[("cacc", j, 0, sn)])
                act(mix[:, 2 + j, cols], cacc[:, j, 0, cols], AF.Silu, reads=[("cacc", j, 0, sn)],
                    writes=[("mix", 2 + j, sn)], scale=cpl[:, j, 33:34], bias=cpl[:, j, 34:35])

    def mixers_tail(l, sts):
        for st in sts:
            n, sn = st.n, st.name
            PB, ZB, CB = hist_bufs(l, st)[:3]
            W = 16 + n
            tPB, tZB, tCB = ("PB", l, sn), ("ZB", l, sn), ("CB", l, sn)
            if st.sample:
                state_out(l, st, 0, PB, 15, 16, tPB, o_pool_s[l])
                state_out(l, st, 1, ZB, 30, 32, tZB, o_conv_s[l])
                state_out(l, st, 2, CB, 2, 2, tCB, o_sc_s[l])
            elif st.last:
                state_out(l, st, 0, PB, 15, 16, tPB, o_pool_p[l, st.seq])
                state_out(l, st, 1, ZB, 30, 32, tZB, o_conv_p[l, st.seq])
                state_out(l, st, 2, CB, 2, 2, tCB, o_sc_p[l, st.seq])
            else:
                cp("pool", PB[:, :, 1:16], PB[:, :, W - 15:W], reads=[tPB], writes=[tPB])
                cp("pool", ZB[:, :, 2:32], ZB[:, :, 32 + n - 30:32 + n], reads=[tZB], writes=[tZB])
                zb_ = zbb_of(l, st)
                cp("pool", zb_[:, :, 2:32], zb_[:, :, 32 + n - 30:32 + n], reads=[("ZBb", l, sn)], writes=[("ZBb", l, sn)])
                cp("pool", CB[:, :, 0:2], CB[:, :, n:n + 2], reads=[tCB], writes=[tCB])

    WOUT_K1 = (0, 1, 4, 5, 6, 7)
    WOUT_K2 = (2, 3)

    def w_out_early(l, sts):
        slots = [next_w("wout", l, 0), next_w("wout", l, 1)]
        for i in range(KD):
            slot, rtok = slots[i // 4]
            ii = i % 4
            for st in sts:
                n, cols, sn = st.n, st.cols, st.name
                b = getbank("mm")
                for kn, k in enumerate(WOUT_K1):
                    mm(ps[b][:, 0:n], slot[:, k, ii * P:(ii + 1) * P], mix[:, k, cols], kn == 0, kn == len(WOUT_K1) - 1,
                       reads=[rtok, ("mix", k, sn)], writes=[pst(b)])
                tt("dve", x[:, i, cols], ps[b][:, 0:n], x[:, i, cols], ALU.add, reads=[pst(b), ("x", i, sn)],
                   writes=[pst(b), ("x", i, sn)])
        return slots

    def w_out_late(l, sts, slots):
        pend = []
        for i in range(KD):
            slot, rtok = slots[i // 4]
            ii = i % 4
            for st in sts:
                n, cols, sn = st.n, st.cols, st.name
                b = getbank("mm")
                for kn, k in enumerate(WOUT_K2):
                    mm(ps[b][:, 0:n], slot[:, k, ii * P:(ii + 1) * P], mix[:, k, cols], kn == 0, kn == len(WOUT_K2) - 1,
                       reads=[rtok, ("mix", k, sn)], writes=[pst(b)])
                tt("dve", x[:, i, cols], ps[b][:, 0:n], x[:, i, cols], ALU.add, reads=[pst(b), ("x", i, sn)],
                   writes=[pst(b), ("x", i, sn)])
                pend.append(norm_sq(st, i))
                flush_keep1(pend)
            if i == 0:
                mixers_tail(l, sts)
        w_advance()
        flush(pend)

    def ffn_stage(l, sts, next_norm, wq=()):
        wq = list(wq)
        for g in range(6):
            nt_ = 4 if g < 5 else 2
            gs, gtok = next_w("wg", l, g)
            us, utok = next_w("wu", l, g)
            pre = {}
            if g == 0:
                st0 = sts[0]
                for jj in range(2):
                    pre[jj] = (getbank("mm"), getbank("mm"))
                for k in range(KD):
                    for jj in range(2):
                        mm(ps[pre[jj][0]][:, 0:st0.n], gs[:, k, jj * P:(jj + 1) * P], hn[:, k, st0.cols], k == 0, k == KD - 1,
                           reads=[gtok, ("hn", k, st0.name)], writes=[pst(pre[jj][0])])
                        mm(ps[pre[jj][1]][:, 0:st0.n], us[:, k, jj * P:(jj + 1) * P], hn[:, k, st0.cols], k == 0, k == KD - 1,
                           reads=[utok, ("hn", k, st0.name)], writes=[pst(pre[jj][1])])
            for jj in range(nt_):
                f = g * 4 + jj
                for st in sts:
                    n, cols, sn = st.n, st.cols, st.name
                    if st is sts[0] and jj in pre:
                        bg, bu = pre[jj]
                    else:
                        bg = getbank("mm")
                        bu = getbank("mm")
                        for k in range(KD):
                            mm(ps[bg][:, 0:n], gs[:, k, jj * P:(jj + 1) * P], hn[:, k, cols], k == 0, k == KD - 1,
                               reads=[gtok, ("hn", k, sn)], writes=[pst(bg)])
                        for k in range(KD):
                            mm(ps[bu][:, 0:n], us[:, k, jj * P:(jj + 1) * P], hn[:, k, cols], k == 0, k == KD - 1,
                               reads=[utok, ("hn", k, sn)], writes=[pst(bu)])
                    r = f % 2
                    act(sg[r][:, 0, 0:n], ps[bg][:, 0:n], AF.Silu, reads=[pst(bg)], writes=[pst(bg), ("sg", r)])
                    tt("dve", hid[:, f, cols], ps[bu][:, 0:n], sg[r][:, 0, 0:n], ALU.mult, reads=[pst(bu), ("sg", r)],
                       writes=[pst(bu), ("hid", f, sn)])
                for _ in range(3):
                    if wq:
                        wq.pop(0)()
            w_advance()
        while wq:
            wq.pop(0)()
        pend = []
        for g in range(4):
            s0, t0 = next_w("wd", l, g * 2)
            s1, t1 = next_w("wd", l, g * 2 + 1)
            for ii in range(2):
                i = g * 2 + ii
                for st in sts:
                    n, cols, sn = st.n, st.cols, st.name
                    b = getbank("mm")
                    for k in range(KF):
                        sl, tk = (s0, t0) if k < 11 else (s1, t1)
                        mm(ps[b][:, 0:n], sl[:, k % 11, ii * P:(ii + 1) * P], hid[:, k, cols], k == 0, k == KF - 1,
                           reads=[tk, ("hid", k, sn)], writes=[pst(b)])
                    tt("dve", x[:, i, cols], ps[b][:, 0:n], x[:, i, cols], ALU.add, reads=[pst(b), ("x", i, sn)],
                       writes=[pst(b), ("x", i, sn)])
                    if next_norm:
                        pend.append(norm_sq(st, i))
                        flush_keep1(pend)
            w_advance()
        flush(pend)

    def issue_x(st):
        if st.sample:
            dma("sp", xs_s[:], x_sample, writes=["xs_s"], key="xload_s")
        else:
            src = x_prompt[st.seq, st.ti * TT:(st.ti + 1) * TT, :].rearrange("(q p) d -> p q d", p=P)
            dma("sp", xs_m[:], src, writes=["xs_m"], key="xload")

    def load_x(st):
        pend = []
        if st.sample:
            b = getbank("mm")
            for k in range(KD):
                tr(ps[b][:, k * NS:(k + 1) * NS], xs_s[0:NS, k * P:(k + 1) * P], ident[0:NS, 0:NS], reads=["xs_s"],
                   writes=[pst(b)])
            cp("dve", x[:, :, st.cols], ps[b][:, 0:KD * NS].rearrange("p (k n) -> p k n", k=KD), reads=[pst(b)],
               writes=[pst(b)] + [("x", k, st.name) for k in range(KD)])
            for k in range(KD):
                pend.append(norm_sq(st, k))
                flush_keep1(pend)
        else:
            for k in range(KD):
                b = getbank("mm")
                for q in range(4):
                    tr(ps[b][:, q * P:(q + 1) * P], xs_m[:, q, k * P:(k + 1) * P], ident[:], reads=["xs_m"],
                       writes=[pst(b)])
                cp("act" if k % 2 else "dve", x[:, k, st.cols], ps[b][:, :], reads=[pst(b)],
                   writes=[pst(b), ("x", k, st.name)])
                pend.append(norm_sq(st, k))
                flush_keep1(pend)
        flush(pend)

    def final_out(st):
        sn = st.name
        act(dmy[:, 0:1], epsb[:, 0:1], AF.Ln, reads=[], writes=["dmy"])
        banks = {}

        def phase_a(q, nq):
            bb = [getbank("mm"), getbank("mm")] if st.sample else [2 * q, 2 * q + 1]
            banks[q] = bb
            for half in range(2):
                b = bb[half]
                for kk in range(4):
                    k = half * 4 + kk
                    tr(ps[b][0:nq, kk * P:(kk + 1) * P], x[:, k, st.c0 + q * P:st.c0 + q * P + nq], ident[:],
                       reads=[("x", k, sn)], writes=[pst(b)])
                c = (q * 2 + half) % 16
                act(sg[half][0:nq, 0, 0:512], ps[b][0:nq, :], AF.Square, reads=[pst(b)], writes=[("sg", half), ("ss", c)],
                    accum_out=ss[0:nq, c:c + 1])

        def phase_b(q, nq):
            bb = banks[q]
            c0_ = (q * 2) % 16
            tt("dve", rs1[0:nq, 0:1], ss[0:nq, c0_:c0_ + 1], ss[0:nq, c0_ + 1:c0_ + 2], ALU.add,
               reads=[("ss", c0_), ("ss", c0_ + 1)], writes=[("rs1", 0)])
            act(rs1[0:nq, 1:2], rs1[0:nq, 0:1], AF.Ln, reads=[("rs1", 0)], writes=[("rs1", 1)], scale=1.0 / D,
                bias=EPS_AP[0][0:nq, :])
            act(rs1[0:nq, 2:3], rs1[0:nq, 1:2], AF.Exp, reads=[("rs1", 1)], writes=[("rs1", 2)], scale=-0.5)
            for half in range(2):
                b = bb[half]
                if st.sample:
                    o = xs_s[0:nq, half * 512:(half + 1) * 512]
                    wtok = "xs_s"
                else:
                    o = ys[:, q, half * 512:(half + 1) * 512]
                    wtok = "ys"
                stt("dve", o, ps[b][0:nq, :], rs1[0:nq, 2:3], gfin[0:nq, half * 512:(half + 1) * 512], ALU.mult, ALU.mult,
                    reads=[pst(b), ("rs1", 2)], writes=[pst(b), wtok])

        blks = st.blocks
        for i_, (q, nq) in enumerate(blks):
            phase_a(q, nq)
            if i_ >= 1:
                phase_b(*blks[i_ - 1])
        phase_b(*blks[-1])
        if st.sample:
            return dma("sp", y_sample, xs_s[:], reads=["xs_s"], key="ystore_s", is_out=True)
        dst = y_prompt[st.seq, st.ti * TT:(st.ti + 1) * TT, :].rearrange("(q p) d -> p q d", p=P)
        return dma("sp", dst, ys[:], reads=["ys"], key="ystore", is_out=True)

    epsb = sb("epsb", [P, 1])
    memset("dve", epsb[:], EPS, writes=["c"])
    EPS_AP = [epsb[:, 0:1]]
    sq_ctr = [0]
    cp("dve", identb[:], ident[:], reads=[], writes=["identb"])
    pg.barrier()
    if with_sample:
        for l in range(DEPTH):
            cp("dve", ZBbs[l][:, :, 2:32], ZBs[l][:, :, 2:32], reads=[("ZB", l, "s")], writes=[("ZBb", l, "s")])
    pg.barrier()

    NOFENCE[0] = True
    w_advance()
    NOFENCE[0] = False
    last_store = None

    def make_sts(pi):
        s_, ti = pi // ntile, pi % ntile
        sts_ = [SubTile("m", TT, 0, False, seq=s_, ti=ti, first=(ti == 0), last=(ti == ntile - 1))]
        if with_sample and pi == 0:
            sts_.append(SubTile("s", NS, TT, True))
        return sts_

    NOFENCE[0] = True
    for st in make_sts(0):
        issue_x(st)
    NOFENCE[0] = False
    for pi in range(n_pass):
        sts = make_sts(pi)
        stm = sts[0]
        for st in sts:
            load_x(st)
        if pi + 1 < n_pass:
            for st in make_sts(pi + 1):
                issue_x(st)
        if stm.first:
            for l in range(DEPTH):
                memset("pool", PBm[l][:, :, 0:16], 0.0, writes=[("PB", l, "m")])
                memset("pool", ZBm[l][:, :, 0:32], 0.0, writes=[("ZB", l, "m")])
                memset("pool", ZBbm[l][:, :, 0:32], 0.0, writes=[("ZBb", l, "m")])
                memset("pool", CBm[l][:, :, 0:2], 0.0, writes=[("CB", l, "m")])
        for l in range(DEPTH):
            for st in sts:
                norm_finish(st, l * 16)
            if l == 0 and last_store is not None:
                pg.fence([last_store])
                last_store = None
            w_in_stage(l, sts, pre_thunks=(wdg_thunks(0) if (pi == 0 and l == 0) else ()))
            mixers_a(l, sts)
            wslots = w_out_early(l, sts)
            mixers_b(l, sts)
            if DEBUG and pi == 0 and l == 0:
                dma("sp", dbg_mix, mix[:, :, 0:TT], reads=[("mix", k, "m") for k in range(KD)], key="dbg", is_out=True)
            w_out_late(l, sts, wslots)
            if DEBUG and pi == 0 and l == 0:
                dma("sp", dbg_x1, x[:, :, 0:TT], reads=[("x", k, "m") for k in range(KD)], key="dbg", is_out=True)
            for st in sts:
                norm_finish(st, l * 16 + 8)
            if DEBUG and pi == 0 and l == 0:
                dma("sp", dbg_hn, hn[:, :, 0:TT], reads=[("hn", k, "m") for k in range(KD)], key="dbg", is_out=True)
            ffn_stage(l, sts, next_norm=(l + 1 < DEPTH), wq=wdg_thunks((l + 1) % DEPTH))
            if DEBUG and pi == 0 and l == 0:
                dma("sp", dbg_x2, x[:, :, 0:TT], reads=[("x", k, "m") for k in range(KD)], key="dbg", is_out=True)
        for st in sts:
            r = final_out(st)
            if not st.sample:
                last_store = r
    pg.op("sp", None, extra=list(pg.out_dmas), waitonly=True)
    pg.emit()
    return nc


def _consts():
    ident = np.eye(P, dtype=np.float32)
    tril = np.tril(np.ones((P, P), np.float32))
    wins = (2, 4, 8, 16)
    rc16 = np.zeros((P, 2, 16), np.float32)
    invw = np.zeros((P, 2), np.float32)
    for j in range(2):
        for p in range(P):
            w = wins[2 * j + p // 64]
            invw[p, j] = 1.0 / w
            for t in range(16):
                rc16[p, j, t] = 1.0 / min(w, t + 1)
    hsel = np.zeros((8, 2, P + 2), np.float32)
    for j in range(2):
        for c in range(P):
            hsel[2 * j + c // 64, j, c] = 1.0
            hsel[4 + 2 * j + c // 64, j, c] = 1.0
    hsel[0:4, :, P] = 1.0
    hsel[4:8, :, P + 1] = 1.0
    return dict(c_ident=ident, c_tril=tril, c_rc16=rc16, c_invw=invw, c_hsel=hsel)


_NC_CACHE = {}


def _run(inputs, n_cores, n_seq, seq_len, with_sample=True, trace=False):
    key = (n_seq, seq_len, with_sample)
    if key not in _NC_CACHE:
        _NC_CACHE[key] = build_program(n_seq, seq_len, with_sample)
    nc = _NC_CACHE[key]
    f = lambda a: np.ascontiguousarray(np.asarray(a, dtype=np.float32))
    cst = _consts()
    pm = np.concatenate([f(inputs["pool_scale"])[:, None, :], f(inputs["w_conf_dw"]), f(inputs["b_conf_dw"])[:, None, :],
                         f(inputs["conf_ln_g"])[:, None, :], f(inputs["conf_ln_b"])[:, None, :], f(inputs["w_sconv"])],
                        axis=1)
    gm = np.concatenate([np.concatenate([f(inputs["g_mix"])[l].reshape(KD, P), f(inputs["g_ffn"])[l].reshape(KD, P)], 0)
                         for l in range(DEPTH)], 0)
    wp = f(inputs["w_pool"])
    pw = np.zeros((P, DEPTH, 2, P), np.float32)
    for l in range(DEPTH):
        for g in range(4):
            j, hh = g // 2, g % 2
            pw[hh * 64:(hh + 1) * 64, l, j, hh * 64:(hh + 1) * 64] = wp[l, g]
    cpack = np.concatenate([cst["c_ident"], cst["c_tril"], cst["c_rc16"].reshape(P, 32), cst["c_invw"]], axis=1)
    shared = dict(
        g_final=f(inputs["g_final"]).reshape(1, D), w_in=f(inputs["w_in"]), w_s=f(inputs["w_s"]),
        w_out=f(inputs["w_out"]), w_gate=f(inputs["w_gate"]), w_up=f(inputs["w_up"]), w_down=f(inputs["w_down"]),
        c_pack=np.ascontiguousarray(cpack), c_hsel=cst["c_hsel"],
        pm_pack=np.ascontiguousarray(pm.transpose(1, 0, 2)), gm_pack=np.ascontiguousarray(gm),
        bs_pack=np.ascontiguousarray(np.concatenate([f(inputs["b_s"]).transpose(1, 0, 2)] * 2, axis=0)), pw_pack=pw)
    xp = f(inputs["x_prompt"])
    xsm = f(inputs["x_sample"])
    in_maps = []
    for c in range(n_cores):
        m = dict(shared)
        m["x_prompt"] = xp[c * n_seq:(c + 1) * n_seq]
        m["x_sample"] = xsm[c]
        sp_ = np.zeros((32, DEPTH, 3, WG), np.float32)
        sp_[0:15, :, 0, :] = f(inputs["state_pool"])[:, c].transpose(1, 0, 2)
        sp_[0:30, :, 1, :] = f(inputs["state_conv"])[:, c].transpose(1, 0, 2)
        sp_[0:2, :, 2, :] = f(inputs["state_short_conv"])[:, c].transpose(1, 0, 2)
        m["st_pack"] = sp_
        in_maps.append(m)
    res = run_bass_kernel_spmd(nc, in_maps, core_ids=list(range(n_cores)), trace=trace)
    R = res.results
    y_prompt = np.concatenate([r["y_prompt"] for r in R], axis=0)
    y_sample = np.stack([r["y_sample"] for r in R], axis=0)
    pool_p = np.concatenate([r["o_pool_p"] for r in R], axis=1)
    pool_s = np.stack([r["o_pool_s"] for r in R], axis=1)
    conv_p = np.concatenate([r["o_conv_p"] for r in R], axis=1)
    conv_s = np.stack([r["o_conv_s"] for r in R], axis=1)
    sc_p = np.concatenate([r["o_sc_p"] for r in R], axis=1)
    sc_s = np.stack([r["o_sc_s"] for r in R], axis=1)
    v_s = np.stack([r["o_v_s"] for r in R], axis=1)
    outs = (y_prompt, y_sample, pool_p, pool_s, conv_p, conv_s, sc_p, sc_s, v_s)
    if DEBUG:
        res.dbg = {k: np.asarray(R[0][k]).astype(np.float32) for k in R[0] if k.startswith('dbg_')}
    return tuple(np.ascontiguousarray(o, dtype=np.float32) for o in outs), res


def kernel(**inputs):
    outs, _ = _run(inputs, 8, 2, 4096, True)
    return outs
```

```python
import contextlib
import numpy as np
import concourse.bass as bass
import concourse.mybir as mybir
from concourse.bass_utils import run_bass_kernel_spmd

F32 = mybir.dt.float32
BF16 = mybir.dt.bfloat16
AF = mybir.ActivationFunctionType
ALU = mybir.AluOpType

P = 128
D = 1024
KD = 8
DIN = 2048
DFF = 2816
KF = 22
WG = 256
DEPTH = 2
EPS = 1e-6
TT = 512
NS = 32
NT = TT + NS
NSLOT = 4
SLOT_ELEMS = 4096
ENGS = ("pe", "act", "dve", "pool", "sp")

NCP = 38


class Ins:
    __slots__ = ("eng", "fn", "deps", "signal", "rank", "idx", "is_dma", "semkey", "semval", "waitonly")

    def __init__(self, eng, fn):
        self.eng = eng
        self.fn = fn
        self.deps = ()
        self.signal = False
        self.rank = 0
        self.idx = 0
        self.is_dma = False
        self.semkey = None
        self.semval = 0
        self.waitonly = False


class Prog:
    def __init__(self, nc):
        self.nc = nc
        self.streams = {e: [] for e in ENGS}
        self.lastw = {}
        self.readers = {}
        self.dma_keys = {}
        self.pending_fence = {e: [] for e in ENGS}
        self.out_dmas = []

    def op(self, eng, fn, reads=(), writes=(), dma_key=None, extra=(), waitonly=False, nofence=False):
        ins = Ins(eng, fn)
        ins.waitonly = waitonly
        deps = set(extra)
        for r in reads:
            w = self.lastw.get(r)
            if w is not None:
                deps.add(w)
        for t in writes:
            w = self.lastw.get(t)
            if w is not None:
                deps.add(w)
            rd = self.readers.get(t)
            if rd:
                deps.update(rd.values())
        if self.pending_fence[eng] and not nofence:
            deps.update(self.pending_fence[eng])
            self.pending_fence[eng] = []
        if dma_key is not None:
            k = self.dma_keys.setdefault(dma_key, [0, None])
            if k[1] is not None:
                deps.add(k[1])
            k[0] += 16
            ins.is_dma = True
            ins.semkey = dma_key
            ins.semval = k[0]
            k[1] = ins
        deps.discard(ins)
        ins.deps = deps
        st = self.streams[eng]
        ins.idx = len(st)
        st.append(ins)
        for r in reads:
            d = self.readers.setdefault(r, {})
            d[("dma", id(ins)) if ins.is_dma else eng] = ins
        for t in writes:
            self.lastw[t] = ins
            self.readers[t] = {}
        return ins

    def fence(self, inss):
        for e in ENGS:
            self.pending_fence[e].extend(inss)

    def barrier(self):
        lasts = []
        for e in ENGS:
            for ins in reversed(self.streams[e]):
                if not ins.is_dma:
                    lasts.append(ins)
                    break
        for name, k in self.dma_keys.items():
            if k[1] is not None and not str(name).startswith(("ring", "xload", "wback")):
                lasts.append(k[1])
        self.fence(lasts)

    @staticmethod
    def _needs_wait(ins, d):
        if d.is_dma:
            return True
        if d.eng == ins.eng:
            return ins.eng != "pe"
        return True

    def emit(self):
        nc = self.nc
        for e in ENGS:
            for ins in self.streams[e]:
                for d in ins.deps:
                    if not d.is_dma and self._needs_wait(ins, d):
                        d.signal = True
        for e in ENGS:
            r = 0
            for ins in self.streams[e]:
                if ins.signal and not ins.is_dma:
                    r += 1
                ins.rank = r
        with contextlib.ExitStack() as es:
            esem = {e: es.enter_context(nc.semaphore("prog_" + e)) for e in ENGS}
            dsem = {k: es.enter_context(nc.semaphore("dma_%d" % i)) for i, k in enumerate(self.dma_keys)}
            block = es.enter_context(nc.Block())

            def run_stream(ename, e):
                waited = {}
                for ins in self.streams[ename]:
                    need = {}
                    for d in ins.deps:
                        if not self._needs_wait(ins, d):
                            continue
                        if d.is_dma:
                            key = ("d", d.semkey)
                            val = d.semval
                        else:
                            key = ("e", d.eng)
                            val = d.rank
                        if val > need.get(key, 0):
                            need[key] = val
                    todo = []
                    for key, val in need.items():
                        if waited.get(key, 0) >= val:
                            continue
                        waited[key] = val
                        todo.append((dsem[key[1]] if key[0] == "d" else esem[key[1]], val))
                    if ins.waitonly:
                        for sem, val in todo:
                            e.wait_ge(sem, val)
                        continue
                    if ins.is_dma:
                        for sem, val in todo:
                            e.wait_ge(sem, val)
                        todo = []
                    for sem, val in todo[1:]:
                        e.wait_ge(sem, val)
                    bi = ins.fn(e)
                    if todo:
                        bi._wait_ge(todo[0][0], todo[0][1])
                    if ins.is_dma:
                        bi.then_inc(dsem[ins.semkey], 16)
                    elif ins.signal:
                        bi.then_inc(esem[ename], 1)

            @block.tensor
            def _(e):
                run_stream("pe", e)

            @block.scalar
            def _(e):
                run_stream("act", e)

            @block.vector
            def _(e):
                run_stream("dve", e)

            @block.gpsimd
            def _(e):
                run_stream("pool", e)

            @block.sync
            def _(e):
                run_stream("sp", e)


class SubTile:
    def __init__(self, name, n, c0, sample, seq=0, ti=0, first=False, last=False):
        self.name = name
        self.n = n
        self.c0 = c0
        self.sample = sample
        self.seq = seq
        self.ti = ti
        self.first = first
        self.last = last
        self.cols = slice(c0, c0 + n)
        self.blocks = [(q, min(P, n - q * P)) for q in range((n + P - 1) // P)]


DEBUG = False


def build_program(n_seq, seq_len, with_sample=True):
    nc = bass.Bass("TRN2", target_bir_lowering=False)
    pg = Prog(nc)
    ntile = seq_len // TT
    assert seq_len % TT == 0

    def din(name, shape, dt=F32):
        return nc.dram_tensor(name, list(shape), dt, kind="ExternalInput").ap()

    def dout(name, shape, dt=F32):
        return nc.dram_tensor(name, list(shape), dt, kind="ExternalOutput").ap()

    x_prompt = din("x_prompt", [n_seq, seq_len, D])
    x_sample = din("x_sample", [NS, D])
    st_pack = din("st_pack", [32, DEPTH, 3, WG])
    g_final = din("g_final", [1, D])
    w_in = din("w_in", [DEPTH, D, DIN])
    w_s = din("w_s", [DEPTH, 4, P, P])
    w_out = din("w_out", [DEPTH, D, D])
    w_gate = din("w_gate", [DEPTH, D, DFF])
    w_up = din("w_up", [DEPTH, D, DFF])
    w_down = din("w_down", [DEPTH, DFF, D])
    c_pack = din("c_pack", [P, 290])
    c_hsel = din("c_hsel", [8, 2, P + 2])
    pm_pack = din("pm_pack", [NCP, DEPTH, WG])
    gm_pack = din("gm_pack", [4 * KD, P])
    bs_pack = din("bs_pack", [8, DEPTH, P])
    pw_pack = din("pw_pack", [P, DEPTH, 2, P])

    y_prompt = dout("y_prompt", [n_seq, seq_len, D])
    y_sample = dout("y_sample", [NS, D])
    o_pool_p = dout("o_pool_p", [DEPTH, n_seq, 15, WG])
    o_pool_s = dout("o_pool_s", [DEPTH, 15, WG])
    o_conv_p = dout("o_conv_p", [DEPTH, n_seq, 30, WG])
    o_conv_s = dout("o_conv_s", [DEPTH, 30, WG])
    o_sc_p = dout("o_sc_p", [DEPTH, n_seq, 2, WG])
    o_sc_s = dout("o_sc_s", [DEPTH, 2, WG])
    o_v_s = dout("o_v_s", [DEPTH, NS, WG])

    if DEBUG:
        dbg_mix = nc.dram_tensor("dbg_mix", [P, KD, TT], BF16, kind="ExternalOutput").ap()
        dbg_x1 = dout("dbg_x1", [P, KD, TT])
        dbg_x2 = dout("dbg_x2", [P, KD, TT])
        dbg_hn = nc.dram_tensor("dbg_hn", [P, KD, TT], BF16, kind="ExternalOutput").ap()

    def dscr(name, shape):
        return nc.dram_tensor(name, list(shape), BF16, kind="Internal").ap()

    WINb = [dscr("WINb%d" % l, [D, DIN]) for l in range(DEPTH)]
    WOUTb = [dscr("WOUTb%d" % l, [D, D]) for l in range(DEPTH)]
    WGb = [dscr("WGb%d" % l, [D, DFF]) for l in range(DEPTH)]
    WUb = [dscr("WUb%d" % l, [D, DFF]) for l in range(DEPTH)]
    WDb = [dscr("WDb%d" % l, [DFF, D]) for l in range(DEPTH)]

    def sb(name, shape, dt=F32):
        return nc.alloc_sbuf_tensor(name, list(shape), dt)

    xs_m = sb("xs_m", [P, 4, D])
    x = sb("x", [P, KD, NT])
    hn = sb("hn", [P, KD, NT], BF16)
    mix = sb("mix", [P, KD, NT], BF16)
    rstd = sb("rstd", [P, NT])
    lnt = sb("lnt", [P, NT])
    sqt = [sb("sqt%d" % i, [P, NT], BF16) for i in range(2)]
    PBm = [sb("PBm%d" % l, [P, 2, 16 + TT]) for l in range(DEPTH)]
    PBs = [sb("PBs%d" % l, [P, 2, 16 + NS]) for l in range(DEPTH)]
    ZBm = [sb("ZBm%d" % l, [P, 2, 32 + TT]) for l in range(DEPTH)]
    ZBs = [sb("ZBs%d" % l, [P, 2, 32 + NS]) for l in range(DEPTH)]
    CBm = [sb("CBm%d" % l, [P, 2, 2 + TT]) for l in range(DEPTH)]
    CBs = [sb("CBs%d" % l, [P, 2, 2 + NS]) for l in range(DEPTH)]
    ring = [sb("ring%d" % i, [P, SLOT_ELEMS], BF16) for i in range(NSLOT)]
    cpk = sb("cpk", [P, 290])
    ident = cpk[:, 0:128]
    tril = cpk[:, 128:256]
    ones_m = sb("ones_m", [P, P], BF16)
    ones_c = sb("ones_c", [P, P], BF16)
    wsT = sb("wsT", [P, DEPTH, 4, P], BF16)
    poolW = sb("poolW", [P, DEPTH, 2, P], BF16)
    CP = sb("CP", [P, DEPTH, 2, NCP])
    G = sb("G", [P, 4 * KD])
    gfin = sb("gfin", [P, D])
    rc16 = cpk[:, 256:288].rearrange("p (j t) -> p j t", j=2)
    invw = cpk[:, 288:290]
    hsel = sb("hsel", [8, 2, P], BF16)
    so = [sb("so%d" % i, [32, WG]) for i in range(4)]
    ss = sb("ss", [P, 16])
    Wdg = sb("Wdg", [P, 62, P], BF16)
    ZBbm = [sb("ZBbm%d" % l, [P, 2, 32 + TT], BF16) for l in range(DEPTH)]
    ZBbs = [sb("ZBbs%d" % l, [P, 2, 32 + NS], BF16) for l in range(DEPTH)]
    identb = sb("identb", [P, P], BF16)
    dmy = sb("dmy", [P, 4])
    rs1 = sb("rs1", [P, 4])
    xs_s = sb("xs_s", [NS, D])

    ARENA = 54400 - 4352
    arena = sb("arena", [P, ARENA // 2], BF16)
    a_off = [0]

    def aview(nbytes_shape, dt, at=None):
        shape = list(nbytes_shape)
        n = int(np.prod(shape))
        esz = 4 if dt == F32 else 2
        nbytes = n * esz
        off = a_off[0] if at is None else at
        off = (off + 31) // 32 * 32
        if at is None:
            a_off[0] = off + nbytes
        assert off + nbytes <= ARENA, (off, nbytes)
        v = arena[:, off // 2:(off + nbytes) // 2]
        if dt == F32:
            v = v.bitcast(F32)
        if len(shape) == 2:
            v = v.rearrange("p (a b) -> p a b", a=shape[0])
        elif len(shape) == 3:
            v = v.rearrange("p (a b c) -> p a b c", a=shape[0], b=shape[1])
        return v

    S2m = aview([2, 16 + TT], F32)
    S4m = aview([2, 16 + TT], F32)
    S8m = aview([1, 16 + TT], F32)
    S16m = aview([1, 16 + TT], F32)
    S2s = aview([2, 16 + NS], F32)
    S4s = aview([2, 16 + NS], F32)
    S8s = aview([1, 16 + NS], F32)
    S16s = aview([1, 16 + NS], F32)
    Dp = aview([2, NT], BF16)
    sig = aview([2, NT], F32)
    cacc = aview([2, 1, NT], F32)
    cbf = aview([2, NT], BF16)
    sqd = aview([2, NT], BF16)
    xsc = aview([2, NT], F32)
    Bsb = aview([2, NT], F32)
    acc3 = aview([2, NT], F32)
    ub = aview([2, NT], F32)
    vtok = aview([5, WG], BF16)
    pt16 = aview([2, 16], F32)
    hid = aview([KF, NT], BF16, at=0)
    sg = [aview([1, NT], F32, at=KF * NT * 2 + 64 + i * (NT * 4 + 32)) for i in range(2)]
    ys_at = KF * NT * 2 + 64 + 2 * (NT * 4 + 32) + 64
    ys = aview([4, D], F32, at=ys_at)
    wsf = aview([DEPTH * 4, P], F32, at=0)
    stg = aview([DEPTH, 3, WG], F32, at=4096)
    poolWf = aview([DEPTH, 2, P], F32, at=4096 + 6144)
    PM = aview([DEPTH, WG], F32, at=4096 + 6144 + 2048)
    hself_a = aview([2, P + 2], F32, at=20480)
    bsf_a = aview([DEPTH, P], F32, at=4096 + 6144 + 2048 + 2048 + 1024)
    bback_a = aview([DEPTH, P], F32, at=4096 + 6144 + 2048 + 2048 + 2048)
    hself = hself_a[0:8, :, :]
    GM_a = aview([1, P], F32, at=4096 + 6144 + 2048 + 2048 + 3072)
    GM = GM_a[0:4 * KD, 0, :]
    bhi = aview([DEPTH, P], BF16, at=4096 + 6144 + 2048 + 2048 + 3072 + 512)[0:8, :, :]
    blo = aview([DEPTH, P], BF16, at=4096 + 6144 + 2048 + 2048 + 3072 + 1024)[0:8, :, :]
    bsf = bsf_a[0:8, :, :]
    bback = bback_a[0:8, :, :]
    bhl4 = sb("bhl4", [8, DEPTH, 4 * P], BF16)

    ps = [nc.alloc_psum_tensor("ps%d" % i, [P, 512], F32) for i in range(8)]
    bank_ctr = {"mm": 0, "st": 0, "aux": 0}
    bank_sets = {"mm": [0, 1, 2, 3], "st": [4, 5], "aux": [6, 7]}

    def getbank(cls):
        s = bank_sets[cls]
        b = s[bank_ctr[cls] % len(s)]
        bank_ctr[cls] += 1
        return b

    def pst(b):
        return ("ps", b)

    misc_ctr = [0]

    def misc_key():
        misc_ctr[0] += 1
        return "misc%d" % (misc_ctr[0] % 64)

    NOFENCE = [False]

    def dma(eng, out, in_, reads=(), writes=(), key=None, is_out=False):
        ins = pg.op(eng, lambda e, o=out, i=in_: e.dma_start(out=o, in_=i), reads=reads, writes=writes,
                    dma_key=key or misc_key(), nofence=NOFENCE[0])
        if is_out:
            pg.out_dmas.append(ins)
        return ins

    def mm(out, lhsT, rhs, start, stop, reads, writes, tile_position=None):
        if tile_position is None:
            fn = lambda e: e.matmul(out, lhsT, rhs, start=start, stop=stop)
        else:
            fn = lambda e: e.matmul(out, lhsT, rhs, start=start, stop=stop, tile_position=tile_position)
        return pg.op("pe", fn, reads=reads, writes=writes)

    def tr(out, in_, idn, reads, writes):
        return pg.op("pe", lambda e: e.transpose(out, in_, idn), reads=reads, writes=writes)

    def act(out, in_, func, reads, writes, scale=None, bias=None, accum_out=None):
        kw = {}
        if scale is not None:
            kw["scale"] = scale
        if bias is not None:
            kw["bias"] = bias
        if accum_out is not None:
            kw["accum_out"] = accum_out
        return pg.op("act", lambda e: e.activation(out=out, in_=in_, func=func, **kw), reads=reads, writes=writes)

    def tt(eng, out, in0, in1, op, reads, writes):
        return pg.op(eng, lambda e: e.tensor_tensor(out=out, in0=in0, in1=in1, op=op), reads=reads, writes=writes)

    def tsc(eng, out, in0, s1, s2, op0, op1, reads, writes):
        if s2 is None:
            fn = lambda e: e.tensor_scalar(out=out, in0=in0, scalar1=s1, scalar2=None, op0=op0)
        else:
            fn = lambda e: e.tensor_scalar(out=out, in0=in0, scalar1=s1, scalar2=s2, op0=op0, op1=op1)
        return pg.op(eng, fn, reads=reads, writes=writes)

    def stt(eng, out, in0, scalar, in1, op0, op1, reads, writes):
        return pg.op(eng, lambda e: e.scalar_tensor_tensor(out=out, in0=in0, scalar=scalar, in1=in1, op0=op0, op1=op1),
                     reads=reads, writes=writes)

    def cp(eng, out, in_, reads, writes):
        if eng == "act":
            return pg.op("act", lambda e: e.copy(out=out, in_=in_), reads=reads, writes=writes)
        return pg.op(eng, lambda e: e.tensor_copy(out=out, in_=in_), reads=reads, writes=writes)

    def memset(eng, ap, val, writes):
        return pg.op(eng, lambda e: e.memset(ap, val), writes=writes)

    dma("sp", cpk[:], c_pack, writes=[("c", 0)])
    dma("act", GM[:, :], gm_pack, writes=[("c", 1)])
    dma("sp", PM[0:NCP, :, :], pm_pack, writes=[("c", 2)])
    dma("act", hself[:], c_hsel, writes=[("c", 3)])
    dma("sp", wsf[:, :, :], w_s.rearrange("l h i j -> i (l h) j"), writes=[("c", 4)])
    dma("act", bsf[:, :, :], bs_pack, writes=[("c", 5)])
    dma("sp", poolWf[:, :, :, :], pw_pack, writes=["pwf"])
    dma("act", gfin[:], g_final.broadcast_to([P, D]), writes=[("c", 6)])
    if with_sample:
        dma("sp", stg[0:32, :, :, :], st_pack, writes=[("c", 7)])
    memset("dve", ones_m[:], 1.0 / 1024.0, writes=["c"])
    memset("dve", ones_c[:], 1.0 / 256.0, writes=["c"])
    pg.barrier()
    cp("dve", poolW[:], poolWf[:], reads=["pwf"], writes=[("c2", 1)])
    cp("dve", hsel[:], hself[:, :, 0:P], reads=[], writes=[("c2", 2)])
    cp("dve", bhi[:], bsf[:], reads=[], writes=["bhi"])
    cp("dve", bback[:], bhi[:], reads=["bhi"], writes=["bback"])
    tt("dve", bback[:], bsf[:], bback[:], ALU.subtract, reads=["bback"], writes=["bback"])
    cp("dve", blo[:], bback[:], reads=["bback"], writes=[("c2", 3)])
    tsc("dve", bback[:], bback[:], hself[:, 0, P + 1:P + 2], None, ALU.mult, None, reads=["bback"], writes=["bback"])
    stt("dve", bback[:], bhi[:], hself[:, 0, P:P + 1], bback[:], ALU.mult, ALU.add, reads=["bback", "bhi"],
        writes=["bback"])
    for r_ in range(4):
        cp("dve", bhl4[:, :, r_ * P:(r_ + 1) * P], bback[:, :, :], reads=["bback"], writes=[("c2", 4)])
    for l in range(DEPTH):
        for h in range(4):
            tt("dve", wsf[:, l * 4 + h, :], wsf[:, l * 4 + h, :], tril[:], ALU.mult, reads=[("wsf", l, h)],
               writes=[("wsf", l, h)])
    for l in range(DEPTH):
        b = getbank("mm")
        for h in range(4):
            tr(ps[b][:, h * P:(h + 1) * P], wsf[:, l * 4 + h, :], ident[:], reads=[("wsf", l, h)], writes=[pst(b)])
        cp("act", wsT[:, l, :, :], ps[b][:, :].rearrange("p (h i) -> p h i", h=4), reads=[pst(b)], writes=[pst(b), ("c2", 5)])
    for l in range(DEPTH):
        b = getbank("mm")
        for j in range(2):
            tr(ps[b][:, j * NCP:(j + 1) * NCP], PM[0:NCP, l, j * P:(j + 1) * P], ident[0:NCP, 0:NCP], reads=[],
               writes=[pst(b)])
        cp("act", CP[:, l, :, :], ps[b][:, 0:2 * NCP].rearrange("p (j c) -> p j c", j=2), reads=[pst(b)],
           writes=[pst(b), ("c2", 6)])
    b = getbank("mm")
    tr(ps[b][:, 0:4 * KD], GM[:, :], ident[0:4 * KD, 0:4 * KD], reads=[], writes=[pst(b)])
    cp("act", G[:], ps[b][:, 0:4 * KD], reads=[pst(b)], writes=[pst(b), ("c2", 7)])
    if with_sample:
        for l in range(DEPTH):
            memset("dve", PBs[l][:, :, 0:1], 0.0, writes=[("PB", l, "s")])
            b = getbank("mm")
            for (kind, nr, c0) in ((0, 15, 0), (1, 30, 64), (2, 2, 128)):
                for j in range(2):
                    tr(ps[b][:, c0 + j * 32:c0 + j * 32 + nr], stg[0:nr, l, kind, j * P:(j + 1) * P],
                       ident[0:nr, 0:nr], reads=[], writes=[pst(b)])
            for j in range(2):
                cp("dve", PBs[l][:, j, 1:16], ps[b][:, j * 32:j * 32 + 15], reads=[pst(b)], writes=[("PB", l, "s")])
                cp("dve", ZBs[l][:, j, 2:32], ps[b][:, 64 + j * 32:64 + j * 32 + 30], reads=[pst(b)],
                   writes=[("ZB", l, "s")])
                cp("act", CBs[l][:, j, 0:2], ps[b][:, 128 + j * 32:128 + j * 32 + 2], reads=[pst(b)],
                   writes=[pst(b), ("CB", l, "s")])
    pg.barrier()

    def layer_stream(l):
        ent = []

        def add(kind, g, fsrc, bsrc, shape):
            ra = lambda ap: ap.rearrange("(k p) e -> p k e", p=P)
            ent.append((kind, l, g, ra(bsrc), shape, ("wbk", kind, l, g), ra(fsrc)))
        for g in (1, 0, 2, 3):
            add("win", g, w_in[l][:, g * 512:(g + 1) * 512], WINb[l][:, g * 512:(g + 1) * 512], (KD, 512))
        for g in range(2):
            add("wout", g, w_out[l][:, g * 512:(g + 1) * 512], WOUTb[l][:, g * 512:(g + 1) * 512], (KD, 512))
        for g in range(6):
            w = 512 if g < 5 else 256
            add("wg", g, w_gate[l][:, g * 512:g * 512 + w], WGb[l][:, g * 512:g * 512 + w], (KD, w))
            add("wu", g, w_up[l][:, g * 512:g * 512 + w], WUb[l][:, g * 512:g * 512 + w], (KD, w))
        for g in range(4):
            for h in range(2):
                add("wd", g * 2 + h, w_down[l][h * 1408:(h + 1) * 1408, g * 256:(g + 1) * 256],
                    WDb[l][h * 1408:(h + 1) * 1408, g * 256:(g + 1) * 256], (11, 256))
        return ent

    n_pass = n_seq * ntile
    stream = []
    for _ in range(n_pass):
        for l in range(DEPTH):
            stream.extend(layer_stream(l))
    per_pass = len(stream) // n_pass
    st_issued = [0]
    st_used = [0]

    def issue_upto(n):
        while st_issued[0] < min(n, len(stream)):
            i = st_issued[0]
            kind, l, g, bsrc, (kk, w), tok, fsrc = stream[i]
            slot = i % NSLOT
            view = ring[slot][:, 0:kk * w].rearrange("p (k e) -> p k e", k=kk)
            if i < per_pass:
                dma("pool", view, fsrc, writes=[("ring", slot)], key="ringc%d" % slot)
                if n_pass > 1:
                    dma("sp", bsrc, view, reads=[("ring", slot)], writes=[tok], key="wback%d" % slot)
            else:
                dma("sp", view, bsrc, reads=[tok], writes=[("ring", slot)], key="ring%d" % slot)
            st_issued[0] += 1

    def next_w(kind, l, g):
        i = st_used[0]
        e = stream[i]
        assert (e[0], e[1], e[2]) == (kind, l, g), (e[:3], kind, l, g)
        assert i < st_issued[0]
        st_used[0] += 1
        slot = i % NSLOT
        kk, w = e[4]
        return ring[slot][:, 0:kk * w].rearrange("p (k e) -> p k e", k=kk), ("ring", slot)

    def w_advance():
        issue_upto(st_used[0] + NSLOT)

    bgq = []
    pp2_pending = [0]

    def drain(n):
        for _ in range(min(n, len(bgq))):
            bgq.pop(0)()

    def drain_all():
        drain(len(bgq))

    nstate = {}

    def norm_sq(st, k):
        n, cols, sn = st.n, st.cols, st.name
        d = nstate.setdefault(sn, {"bank": None, "cnt": 0})
        if d["cnt"] == 0:
            d["bank"] = getbank("st")
            act(dmy[:, 0:1], epsb[:, 0:1], AF.Ln, reads=[], writes=["dmy"])
        i = d["cnt"]
        d["cnt"] += 1
        r = sq_ctr[0] % 2
        sq_ctr[0] += 1
        s = sqt[r]
        act(s[:, 0:n], x[:, k, cols], AF.Square, reads=[("x", k, sn)], writes=[("sqt", r)])
        b = d["bank"]

        def thunk():
            mm(ps[b][:, 0:n], ones_m[:], s[:, 0:n], i == 0, i == KD - 1, reads=[("sqt", r)], writes=[pst(b)])
        return thunk

    def norm_finish(st, gcol):
        n, cols, sn = st.n, st.cols, st.name
        d = nstate.pop(sn)
        assert d["cnt"] == KD
        b = d["bank"]
        act(lnt[:, cols], ps[b][:, 0:n], AF.Ln, reads=[pst(b)], writes=[pst(b), ("lnt", sn)], bias=EPS_AP[0])
        act(rstd[:, cols], lnt[:, cols], AF.Exp, reads=[("lnt", sn)], writes=[("rstd", sn)], scale=-0.5)
        for k in range(KD):
            stt("dve", hn[:, k, cols], x[:, k, cols], G[:, gcol + k:gcol + k + 1], rstd[:, cols], ALU.mult, ALU.mult,
                reads=[("x", k, sn), ("rstd", sn)], writes=[("hn", k, sn)])

    def flush_keep1(pend):
        while len(pend) > 1:
            pend.pop(0)()

    def flush(pend):
        while pend:
            pend.pop(0)()

    def hist_bufs(l, st):
        if st.sample:
            return PBs[l], ZBs[l], CBs[l], S2s, S4s, S8s, S16s
        return PBm[l], ZBm[l], CBm[l], S2m, S4m, S8m, S16m

    def pool_part1(l, st):
        n, cols, sn = st.n, st.cols, st.name
        PB, ZB, CB, S2, S4, S8, S16 = hist_bufs(l, st)
        W = 16 + n
        tPB = ("PB", l, sn)
        e_ = "pool"
        tt(e_, S2[:, :, 1:W], PB[:, :, 1:W], PB[:, :, 0:W - 1], ALU.add, reads=[tPB], writes=[("S2", sn)])
        tt(e_, S4[:, :, 3:W], S2[:, :, 3:W], S2[:, :, 1:W - 2], ALU.add, reads=[("S2", sn)], writes=[("S4", sn)])
        tt(e_, S8[:, 0, 7:W], S4[:, 1, 7:W], S4[:, 1, 3:W - 4], ALU.add, reads=[("S4", sn)], writes=[("S8", sn)])
        tt(e_, S16[64:128, 0, 15:W], S8[64:128, 0, 15:W], S8[64:128, 0, 7:W - 8], ALU.add, reads=[("S8", sn)],
           writes=[("S16", sn)])
        grp = ((0, 0, 64, S2[0:64, 0, :], ("S2", sn)), (0, 64, 128, S4[64:128, 0, :], ("S4", sn)),
               (1, 0, 64, S8[0:64, 0, :], ("S8", sn)), (1, 64, 128, S16[64:128, 0, :], ("S16", sn)))
        for gi, (j, p0, p1, sv, stok) in enumerate(grp):
            stt("dve", Dp[p0:p1, j, cols], sv[:, 16:W], invw[p0:p1, j:j + 1], PB[p0:p1, j, 16:W],
                ALU.mult, ALU.subtract, reads=[stok, tPB], writes=[("Dp", j, sn)])
        if st.first and not st.sample:
            for (j, p0, p1, sv, stok) in grp:
                tt("dve", pt16[p0:p1, j, :], sv[:, 16:32], rc16[p0:p1, j, :], ALU.mult, reads=[stok],
                   writes=[("pt16", j, p0)])
                tt("dve", Dp[p0:p1, j, st.c0:st.c0 + 16], pt16[p0:p1, j, :], PB[p0:p1, j, 16:32], ALU.subtract,
                   reads=[("pt16", j, p0), tPB], writes=[("Dp", j, sn)])

    def pool_part2(l, st):
        n, cols, sn = st.n, st.cols, st.name
        for j in range(2):
            b = getbank("mm")
            mm(ps[b][:, 0:n], poolW[:, l, j, :], Dp[:, j, cols], True, True, reads=[("Dp", j, sn)], writes=[pst(b)])
            tsc("dve", mix[:, j, cols], ps[b][:, 0:n], CP[:, l, j, 0:1], None, ALU.mult, None, reads=[pst(b)],
                writes=[pst(b), ("mix", j, sn)])

    def zbb_of(l, st):
        return ZBbs[l] if st.sample else ZBbm[l]

    def conv_thunks(l, st):
        n, cols, sn = st.n, st.cols, st.name
        ZBb = zbb_of(l, st)
        tzb = ("ZBb", l, sn)
        cpl = CP[:, l, :, :]
        out = []

        def tile(j):
            b = getbank("mm")
            for k in range(31):
                mm(ps[b][:, 0:n], Wdg[:, j * 31 + k, :], ZBb[:, j, 2 + k:2 + k + n], k == 0, k == 30,
                   reads=[tzb, ("Wdg", j * 31 + k)], writes=[pst(b)])
            act(cacc[:, j, 0, cols], ps[b][:, 0:n], AF.Identity, reads=[pst(b)], writes=[pst(b), ("cacc", j, 0, sn)],
                bias=cpl[:, j, 32:33])
        for j in range(2):
            out.append(lambda j=j: tile(j))
        return out

    def wdg_thunks(l):
        out = []
        for j in range(2):
            for k in range(31):
                out.append(lambda j=j, k=k: act(Wdg[:, j * 31 + k, :], identb[:], AF.Identity, reads=[],
                                                writes=[("Wdg", j * 31 + k)], scale=CP[:, l, j, 1 + k:2 + k]))
        return out

    WIN_ORDER = {1: (4, 5, 6, 7), 0: (0, 1, 2, 3), 2: (8, 9, 10, 11), 3: (12, 13)}

    def w_in_stage(l, sts, pre_thunks=()):
        npre = [len(pre_thunks)]
        bgq.extend(pre_thunks)

        def drain_step():
            if npre[0] > 0:
                k_ = min(6, npre[0])
                npre[0] -= k_
                drain(k_)
            else:
                drain(1)
        for g in (1, 0, 2, 3):
            slot, rtok = next_w("win", l, g)
            pre = {}
            if g == 1:
                st0 = sts[0]
                for j in WIN_ORDER[g]:
                    pre[j] = getbank("mm")
                for k in range(KD):
                    for j in WIN_ORDER[g]:
                        mm(ps[pre[j]][:, 0:st0.n], slot[:, k, (j % 4) * P:(j % 4 + 1) * P], hn[:, k, st0.cols], k == 0,
                           k == KD - 1, reads=[rtok, ("hn", k, st0.name)], writes=[pst(pre[j])])
            for j in WIN_ORDER[g]:
                jj = j % 4
                for st in sts:
                    n, cols, sn = st.n, st.cols, st.name
                    PB, ZB, CB = hist_bufs(l, st)[:3]
                    if st is sts[0] and j in pre:
                        b = pre[j]
                    else:
                        b = getbank("mm")
                        for k in range(KD):
                            mm(ps[b][:, 0:n], slot[:, k, jj * P:(jj + 1) * P], hn[:, k, cols], k == 0, k == KD - 1,
                               reads=[rtok, ("hn", k, sn)], writes=[pst(b)])
                    src = ps[b][:, 0:n]
                    if j in (0, 1):
                        cp("act", PB[:, j, 16:16 + n], src, reads=[pst(b)], writes=[pst(b), ("PB", l, sn)])
                    elif j in (2, 3):
                        tt("dve", ZB[:, j - 2, 32:32 + n], src, sig[:, j - 2, cols], ALU.mult,
                           reads=[pst(b), ("sig", j - 2, sn)], writes=[pst(b), ("ZB", l, sn)])
                        cp("act", zbb_of(l, st)[:, j - 2, 32:32 + n], ZB[:, j - 2, 32:32 + n], reads=[("ZB", l, sn)],
                           writes=[("ZBb", l, sn)])
                    elif j in (4, 5):
                        act(sig[:, j - 4, cols], src, AF.Sigmoid, reads=[pst(b)], writes=[pst(b), ("sig", j - 4, sn)])
                    elif j in (6, 7):
                        cp("act", xsc[:, j - 6, cols], src, reads=[pst(b)], writes=[pst(b), ("xsc", j - 6, sn)])
                    elif j in (8, 9):
                        cp("act", Bsb[:, j - 8, cols], src, reads=[pst(b)], writes=[pst(b), ("Bsb", j - 8, sn)])
                    elif j in (10, 11):
                        tt("dve", CB[:, j - 10, 2:2 + n], src, xsc[:, j - 10, cols], ALU.mult,
                           reads=[pst(b), ("xsc", j - 10, sn)], writes=[pst(b), ("CB", l, sn)])
                    elif j in (12, 13):
                        cp("act", ub[:, j - 12, cols], src, reads=[pst(b)], writes=[pst(b), ("u", j - 12, sn)])
                    drain_step()
                if j == 5:
                    act(dmy[:, 0:1], epsb[:, 0:1], AF.Ln, reads=[], writes=["dmy"])
                if j == 3:
                    if npre[0] > 0:
                        drain(npre[0])
                        npre[0] = 0
                    for st in sts:
                        pool_part1(l, st)
                    for st in sts:
                        th = conv_thunks(l, st)
                        bgq.append(th[0])
                        pp2_pending[0] += 1
                        bgq.append(lambda st=st: (pool_part2(l, st), pp2_pending.__setitem__(0, pp2_pending[0] - 1)))
                        bgq.append(th[1])
            if g == 3:
                for st in sts:
                    sn = st.name
                    bb = [getbank("aux"), getbank("aux")]
                    for (q, nq) in st.blocks:
                        b = bb[q // 2]
                        o = ps[b][0:nq, (q % 2) * WG:(q % 2 + 1) * WG]
                        for k in range(KD):
                            mm(o, hn[:, k, st.c0 + q * P:st.c0 + q * P + nq], slot[:, k, 256:512], k == 0, k == KD - 1,
                               reads=[rtok, ("hn", k, sn)], writes=[pst(b)])
                        drain_step()
                    if st.sample:
                        cp("act", vtok[0:NS, 4, :], ps[bb[0]][0:NS, 0:WG], reads=[pst(bb[0])], writes=[("vtok", sn)])
                        cp("dve", so[3][0:NS, :], ps[bb[0]][0:NS, 0:WG], reads=[pst(bb[0])],
                           writes=[pst(bb[0]), ("so", 3)])
                        dma("sp", o_v_s[l], so[3][0:NS, :], reads=[("so", 3)], key="so3", is_out=True)
                    else:
                        for hb in range(2):
                            cp("act", vtok[:, 2 * hb:2 * hb + 2, :], ps[bb[hb]][:, :].rearrange("p (q c) -> p q c", q=2),
                               reads=[pst(bb[hb])], writes=[pst(bb[hb]), ("vtok", sn)])
            w_advance()

    def state_out(l, st, kind, buf, nr, hist, tok, dst):
        n, sn = st.n, st.name
        W = hist + n
        b = getbank("mm")
        for j in range(2):
            tr(ps[b][0:nr, j * P:(j + 1) * P], buf[:, j, W - nr:W], ident[:], reads=[tok], writes=[pst(b)])
        cp("dve", so[kind][0:nr, :], ps[b][0:nr, 0:WG], reads=[pst(b)], writes=[pst(b), ("so", kind)])
        dma("sp", dst, so[kind][0:nr, :], reads=[("so", kind)], key="so%d" % kind, is_out=True)

    def mixers_a(l, sts):
        while pp2_pending[0] > 0:
            drain(1)
        cpl = CP[:, l, :, :]
        for st in sts:
            n, cols, sn = st.n, st.cols, st.name
            PB, ZB, CB, S2, S4, S8, S16 = hist_bufs(l, st)
            tCB = ("CB", l, sn)
            for k in range(3):
                for j in range(2):
                    if k == 0:
                        tsc("dve", acc3[:, j, cols], CB[:, j, 0:n], cpl[:, j, 35:36], None, ALU.mult, None, reads=[tCB],
                            writes=[("acc3", j, sn)])
                    else:
                        stt("dve", acc3[:, j, cols], CB[:, j, k:k + n], cpl[:, j, 35 + k:36 + k], acc3[:, j, cols],
                            ALU.mult, ALU.add, reads=[tCB, ("acc3", j, sn)], writes=[("acc3", j, sn)])
            for j in range(2):
                tt("dve", mix[:, 4 + j, cols], acc3[:, j, cols], Bsb[:, j, cols], ALU.mult,
                   reads=[("acc3", j, sn), ("Bsb", j, sn)], writes=[("mix", 4 + j, sn)])
            for j in range(2):
                bq = getbank("aux")
                nblk = len(st.blocks)
                for hh in range(2):
                    tp = (0, 64) if hh else None
                    h_ = 2 * j + hh
                    o_all = ps[bq][hh * 64:(hh + 1) * 64, 0:n]
                    mm(o_all, hsel[:, j, hh * 64:(hh + 1) * 64], bhl4[:, l, 0:n], True, False, reads=[("vtok", sn)],
                       writes=[pst(bq)], tile_position=tp)
                    for bi_, (q, nq) in enumerate(st.blocks):
                        qi = 4 if st.sample else q
                        o = ps[bq][hh * 64:(hh + 1) * 64, q * P:q * P + nq]
                        mm(o, vtok[0:nq, qi, h_ * 64:(h_ + 1) * 64], wsT[0:nq, l, h_, 0:nq], False, bi_ == nblk - 1,
                           reads=[("vtok", sn)], writes=[pst(bq)], tile_position=tp)
                tt("dve", mix[:, 6 + j, cols], ps[bq][:, 0:n], ub[:, j, cols], ALU.mult, reads=[pst(bq), ("u", j, sn)],
                   writes=[pst(bq), ("mix", 6 + j, sn)])

    def mixers_b(l, sts):
        drain_all()
        cpl = CP[:, l, :, :]
        for st in sts:
            n, cols, sn = st.n, st.cols, st.name
            b = getbank("st")
            for j in range(2):
                cp("act", cbf[:, j, cols], cacc[:, j, 0, cols], reads=[("cacc", j, 0, sn)], writes=[("cbf", j, sn)])
                mm(ps[b][:, 0:n], ones_c[:], cbf[:, j, cols], j == 0, j == 1, reads=[("cbf", j, sn)], writes=[pst(b)])
            b2 = getbank("st")
            for j in range(2):
                tt("dve", cacc[:, j, 0, cols], cacc[:, j, 0, cols], ps[b][:, 0:n], ALU.subtract,
                   reads=[("cacc", j, 0, sn), pst(b)], writes=[("cacc", j, 0, sn)] + ([pst(b)] if j == 1 else []))
                act(sqd[:, j, cols], cacc[:, j, 0, cols], AF.Square, reads=[("cacc", j, 0, sn)], writes=[("sqd", j, sn)])
                mm(ps[b2][:, 0:n], ones_c[:], sqd[:, j, cols], j == 0, j == 1, reads=[("sqd", j, sn)], writes=[pst(b2)])
            act(lnt[:, cols], ps[b2][:, 0:n], AF.Ln, reads=[pst(b2)], writes=[pst(b2), ("lnt", sn)], bias=EPS_AP[0])
            act(rstd[:, cols], lnt[:, cols], AF.Exp, reads=[("lnt", sn)], writes=[("rstd", sn)], scale=-0.5)
            for j in range(2):
                tt("dve", cacc[:, j, 0, cols], cacc[:, j, 0, cols], rstd[:, cols], ALU.mult,
                   reads=[("cacc", j, 0, sn), ("rstd", sn)], writes=[("cacc", j, 0, sn)])
                act(mix[:, 2 + j, cols], cacc[:, j, 0, cols], AF.Silu, reads=[("cacc", j, 0, sn)],
                    writes=[("mix", 2 + j, sn)], scale=cpl[:, j, 33:34], bias=cpl[:, j, 34:35])

    def mixers_tail(l, sts):
        for st in sts:
            n, sn = st.n, st.name
            PB, ZB, CB = hist_bufs(l, st)[:3]
            W = 16 + n
            tPB, tZB, tCB = ("PB", l, sn), ("ZB", l, sn), ("CB", l, sn)
            if st.sample:
                state_out(l, st, 0, PB, 15, 16, tPB, o_pool_s[l])
                state_out(l, st, 1, ZB, 30, 32, tZB, o_conv_s[l])
                state_out(l, st, 2, CB, 2, 2, tCB, o_sc_s[l])
            elif st.last:
                state_out(l, st, 0, PB, 15, 16, tPB, o_pool_p[l, st.seq])
                state_out(l, st, 1, ZB, 30, 32, tZB, o_conv_p[l, st.seq])
                state_out(l, st, 2, CB, 2, 2, tCB, o_sc_p[l, st.seq])
            else:
                cp("pool", PB[:, :, 1:16], PB[:, :, W - 15:W], reads=[tPB], writes=[tPB])
                cp("pool", ZB[:, :, 2:32], ZB[:, :, 32 + n - 30:32 + n], reads=[tZB], writes=[tZB])
                zb_ = zbb_of(l, st)
                cp("pool", zb_[:, :, 2:32], zb_[:, :, 32 + n - 30:32 + n], reads=[("ZBb", l, sn)], writes=[("ZBb", l, sn)])
                cp("pool", CB[:, :, 0:2], CB[:, :, n:n + 2], reads=[tCB], writes=[tCB])

    WOUT_K1 = (0, 1, 4, 5, 6, 7)
    WOUT_K2 = (2, 3)

    def w_out_early(l, sts):
        slots = [next_w("wout", l, 0), next_w("wout", l, 1)]
        for i in range(KD):
            slot, rtok = slots[i // 4]
            ii = i % 4
            for st in sts:
                n, cols, sn = st.n, st.cols, st.name
                b = getbank("mm")
                for kn, k in enumerate(WOUT_K1):
                    mm(ps[b][:, 0:n], slot[:, k, ii * P:(ii + 1) * P], mix[:, k, cols], kn == 0, kn == len(WOUT_K1) - 1,
                       reads=[rtok, ("mix", k, sn)], writes=[pst(b)])
                tt("dve", x[:, i, cols], ps[b][:, 0:n], x[:, i, cols], ALU.add, reads=[pst(b), ("x", i, sn)],
                   writes=[pst(b), ("x", i, sn)])
        return slots

    def w_out_late(l, sts, slots):
        pend = []
        for i in range(KD):
            slot, rtok = slots[i // 4]
            ii = i % 4
            for st in sts:
                n, cols, sn = st.n, st.cols, st.name
                b = getbank("mm")
                for kn, k in enumerate(WOUT_K2):
                    mm(ps[b][:, 0:n], slot[:, k, ii * P:(ii + 1) * P], mix[:, k, cols], kn == 0, kn == len(WOUT_K2) - 1,
                       reads=[rtok, ("mix", k, sn)], writes=[pst(b)])
                tt("dve", x[:, i, cols], ps[b][:, 0:n], x[:, i, cols], ALU.add, reads=[pst(b), ("x", i, sn)],
                   writes=[pst(b), ("x", i, sn)])
                pend.append(norm_sq(st, i))
                flush_keep1(pend)
            if i == 0:
                mixers_tail(l, sts)
        w_advance()
        flush(pend)

    def ffn_stage(l, sts, next_norm, wq=()):
        wq = list(wq)
        for g in range(6):
            nt_ = 4 if g < 5 else 2
            gs, gtok = next_w("wg", l, g)
            us, utok = next_w("wu", l, g)
            pre = {}
            if g == 0:
                st0 = sts[0]
                for jj in range(2):
                    pre[jj] = (getbank("mm"), getbank("mm"))
                for k in range(KD):
                    for jj in range(2):
                        mm(ps[pre[jj][0]][:, 0:st0.n], gs[:, k, jj * P:(jj + 1) * P], hn[:, k, st0.cols], k == 0, k == KD - 1,
                           reads=[gtok, ("hn", k, st0.name)], writes=[pst(pre[jj][0])])
                        mm(ps[pre[jj][1]][:, 0:st0.n], us[:, k, jj * P:(jj + 1) * P], hn[:, k, st0.cols], k == 0, k == KD - 1,
                           reads=[utok, ("hn", k, st0.name)], writes=[pst(pre[jj][1])])
            for jj in range(nt_):
                f = g * 4 + jj
                for st in sts:
                    n, cols, sn = st.n, st.cols, st.name
                    if st is sts[0] and jj in pre:
                        bg, bu = pre[jj]
                    else:
                        bg = getbank("mm")
                        bu = getbank("mm")
                        for k in range(KD):
                            mm(ps[bg][:, 0:n], gs[:, k, jj * P:(jj + 1) * P], hn[:, k, cols], k == 0, k == KD - 1,
                               reads=[gtok, ("hn", k, sn)], writes=[pst(bg)])
                        for k in range(KD):
                            mm(ps[bu][:, 0:n], us[:, k, jj * P:(jj + 1) * P], hn[:, k, cols], k == 0, k == KD - 1,
                               reads=[utok, ("hn", k, sn)], writes=[pst(bu)])
                    r = f % 2
                    act(sg[r][:, 0, 0:n], ps[bg][:, 0:n], AF.Silu, reads=[pst(bg)], writes=[pst(bg), ("sg", r)])
                    tt("dve", hid[:, f, cols], ps[bu][:, 0:n], sg[r][:, 0, 0:n], ALU.mult, reads=[pst(bu), ("sg", r)],
                       writes=[pst(bu), ("hid", f, sn)])
                for _ in range(3):
                    if wq:
                        wq.pop(0)()
            w_advance()
        while wq:
            wq.pop(0)()
        pend = []
        for g in range(4):
            s0, t0 = next_w("wd", l, g * 2)
            s1, t1 = next_w("wd", l, g * 2 + 1)
            for ii in range(2):
                i = g * 2 + ii
                for st in sts:
                    n, cols, sn = st.n, st.cols, st.name
                    b = getbank("mm")
                    for k in range(KF):
                        sl, tk = (s0, t0) if k < 11 else (s1, t1)
                        mm(ps[b][:, 0:n], sl[:, k % 11, ii * P:(ii + 1) * P], hid[:, k, cols], k == 0, k == KF - 1,
                           reads=[tk, ("hid", k, sn)], writes=[pst(b)])
                    tt("dve", x[:, i, cols], ps[b][:, 0:n], x[:, i, cols], ALU.add, reads=[pst(b), ("x", i, sn)],
                       writes=[pst(b), ("x", i, sn)])
                    if next_norm:
                        pend.append(norm_sq(st, i))
                        flush_keep1(pend)
            w_advance()
        flush(pend)

    def issue_x(st):
        if st.sample:
            dma("sp", xs_s[:], x_sample, writes=["xs_s"], key="xload_s")
        else:
            src = x_prompt[st.seq, st.ti * TT:(st.ti + 1) * TT, :].rearrange("(q p) d -> p q d", p=P)
            dma("sp", xs_m[:], src, writes=["xs_m"], key="xload")

    def load_x(st):
        pend = []
        if st.sample:
            b = getbank("mm")
            for k in range(KD):
                tr(ps[b][:, k * NS:(k + 1) * NS], xs_s[0:NS, k * P:(k + 1) * P], ident[0:NS, 0:NS], reads=["xs_s"],
                   writes=[pst(b)])
            cp("dve", x[:, :, st.cols], ps[b][:, 0:KD * NS].rearrange("p (k n) -> p k n", k=KD), reads=[pst(b)],
               writes=[pst(b)] + [("x", k, st.name) for k in range(KD)])
            for k in range(KD):
                pend.append(norm_sq(st, k))
                flush_keep1(pend)
        else:
            for th in load_x_thunks(st):
                th()
            return
        flush(pend)

    def load_x_thunks(st):
        pend = []

        def chunk(k):
            b = getbank("mm")
            for q in range(4):
                tr(ps[b][:, q * P:(q + 1) * P], xs_m[:, q, k * P:(k + 1) * P], ident[:], reads=["xs_m"],
                   writes=[pst(b)])
            cp("act" if k % 2 else "dve", x[:, k, st.cols], ps[b][:, :], reads=[pst(b)],
               writes=[pst(b), ("x", k, st.name)])
            pend.append(norm_sq(st, k))
            flush_keep1(pend)
        out = [(lambda k=k: chunk(k)) for k in range(KD)]
        out.append(lambda: flush(pend))
        return out

    def final_out(st, inter=None):
        inter = list(inter) if inter else []
        sn = st.name
        act(dmy[:, 0:1], epsb[:, 0:1], AF.Ln, reads=[], writes=["dmy"])
        banks = {}

        def phase_a(q, nq):
            bb = [getbank("mm"), getbank("mm")] if st.sample else [2 * q, 2 * q + 1]
            banks[q] = bb
            for half in range(2):
                b = bb[half]
                for kk in range(4):
                    k = half * 4 + kk
                    tr(ps[b][0:nq, kk * P:(kk + 1) * P], x[:, k, st.c0 + q * P:st.c0 + q * P + nq], ident[:],
                       reads=[("x", k, sn)], writes=[pst(b)])
                c = (q * 2 + half) % 16
                act(sg[half][0:nq, 0, 0:512], ps[b][0:nq, :], AF.Square, reads=[pst(b)], writes=[("sg", half), ("ss", c)],
                    accum_out=ss[0:nq, c:c + 1])

        def phase_b(q, nq):
            bb = banks[q]
            c0_ = (q * 2) % 16
            tt("dve", rs1[0:nq, 0:1], ss[0:nq, c0_:c0_ + 1], ss[0:nq, c0_ + 1:c0_ + 2], ALU.add,
               reads=[("ss", c0_), ("ss", c0_ + 1)], writes=[("rs1", 0)])
            act(rs1[0:nq, 1:2], rs1[0:nq, 0:1], AF.Ln, reads=[("rs1", 0)], writes=[("rs1", 1)], scale=1.0 / D,
                bias=EPS_AP[0][0:nq, :])
            act(rs1[0:nq, 2:3], rs1[0:nq, 1:2], AF.Exp, reads=[("rs1", 1)], writes=[("rs1", 2)], scale=-0.5)
            for half in range(2):
                b = bb[half]
                if st.sample:
                    o = xs_s[0:nq, half * 512:(half + 1) * 512]
                    wtok = "xs_s"
                else:
                    o = ys[:, q, half * 512:(half + 1) * 512]
                    wtok = "ys"
                stt("dve", o, ps[b][0:nq, :], rs1[0:nq, 2:3], gfin[0:nq, half * 512:(half + 1) * 512], ALU.mult, ALU.mult,
                    reads=[pst(b), ("rs1", 2)], writes=[pst(b), wtok])

        blks = st.blocks
        for i_, (q, nq) in enumerate(blks):
            phase_a(q, nq)
            last_ = (i_ == len(blks) - 1)
            if last_ and inter:
                inter.pop(0)()
            if i_ >= 1:
                phase_b(*blks[i_ - 1])
                if last_ and inter:
                    inter.pop(0)()
        phase_b(*blks[-1])
        while inter:
            inter.pop(0)()
        if st.sample:
            return dma("sp", y_sample, xs_s[:], reads=["xs_s"], key="ystore_s", is_out=True)
        dst = y_prompt[st.seq, st.ti * TT:(st.ti + 1) * TT, :].rearrange("(q p) d -> p q d", p=P)
        return dma("sp", dst, ys[:], reads=["ys"], key="ystore", is_out=True)

    epsb = sb("epsb", [P, 1])
    memset("dve", epsb[:], EPS, writes=["c"])
    EPS_AP = [epsb[:, 0:1]]
    sq_ctr = [0]
    cp("dve", identb[:], ident[:], reads=[], writes=["identb"])
    pg.barrier()
    if with_sample:
        for l in range(DEPTH):
            cp("dve", ZBbs[l][:, :, 2:32], ZBs[l][:, :, 2:32], reads=[("ZB", l, "s")], writes=[("ZBb", l, "s")])
    pg.barrier()

    NOFENCE[0] = True
    w_advance()
    NOFENCE[0] = False
    last_store = None

    def make_sts(pi):
        s_, ti = pi // ntile, pi % ntile
        sts_ = [SubTile("m", TT, 0, False, seq=s_, ti=ti, first=(ti == 0), last=(ti == ntile - 1))]
        if with_sample and pi == 0:
            sts_.append(SubTile("s", NS, TT, True))
        return sts_

    NOFENCE[0] = True
    for st in make_sts(0):
        issue_x(st)
    NOFENCE[0] = False
    preloaded = False
    for pi in range(n_pass):
        sts = make_sts(pi)
        stm = sts[0]
        for st in sts:
            if st.sample or not preloaded:
                load_x(st)
        if pi + 1 < n_pass:
            for st in make_sts(pi + 1):
                issue_x(st)
        if stm.first:
            for l in range(DEPTH):
                memset("pool", PBm[l][:, :, 0:16], 0.0, writes=[("PB", l, "m")])
                memset("pool", ZBm[l][:, :, 0:32], 0.0, writes=[("ZB", l, "m")])
                memset("pool", ZBbm[l][:, :, 0:32], 0.0, writes=[("ZBb", l, "m")])
                memset("pool", CBm[l][:, :, 0:2], 0.0, writes=[("CB", l, "m")])
        for l in range(DEPTH):
            for st in sts:
                norm_finish(st, l * 16)
            if l == 0 and last_store is not None:
                pg.fence([last_store])
                last_store = None
            w_in_stage(l, sts, pre_thunks=(wdg_thunks(0) if (pi == 0 and l == 0) else ()))
            mixers_a(l, sts)
            wslots = w_out_early(l, sts)
            mixers_b(l, sts)
            if DEBUG and pi == 0 and l == 0:
                dma("sp", dbg_mix, mix[:, :, 0:TT], reads=[("mix", k, "m") for k in range(KD)], key="dbg", is_out=True)
            w_out_late(l, sts, wslots)
            if DEBUG and pi == 0 and l == 0:
                dma("sp", dbg_x1, x[:, :, 0:TT], reads=[("x", k, "m") for k in range(KD)], key="dbg", is_out=True)
            for st in sts:
                norm_finish(st, l * 16 + 8)
            if DEBUG and pi == 0 and l == 0:
                dma("sp", dbg_hn, hn[:, :, 0:TT], reads=[("hn", k, "m") for k in range(KD)], key="dbg", is_out=True)
            ffn_stage(l, sts, next_norm=(l + 1 < DEPTH), wq=wdg_thunks((l + 1) % DEPTH))
            if DEBUG and pi == 0 and l == 0:
                dma("sp", dbg_x2, x[:, :, 0:TT], reads=[("x", k, "m") for k in range(KD)], key="dbg", is_out=True)
        for st in sts:
            if not st.sample and pi + 1 < n_pass:
                r = final_out(st, inter=load_x_thunks(make_sts(pi + 1)[0]))
                preloaded = True
            else:
                r = final_out(st)
            if not st.sample:
                last_store = r
    pg.op("sp", None, extra=list(pg.out_dmas), waitonly=True)
    pg.emit()
    return nc


def _consts():
    ident = np.eye(P, dtype=np.float32)
    tril = np.tril(np.ones((P, P), np.float32))
    wins = (2, 4, 8, 16)
    rc16 = np.zeros((P, 2, 16), np.float32)
    invw = np.zeros((P, 2), np.float32)
    for j in range(2):
        for p in range(P):
            w = wins[2 * j + p // 64]
            invw[p, j] = 1.0 / w
            for t in range(16):
                rc16[p, j, t] = 1.0 / min(w, t + 1)
    hsel = np.zeros((8, 2, P + 2), np.float32)
    for j in range(2):
        for c in range(P):
            hsel[2 * j + c // 64, j, c] = 1.0
            hsel[4 + 2 * j + c // 64, j, c] = 1.0
    hsel[0:4, :, P] = 1.0
    hsel[4:8, :, P + 1] = 1.0
    return dict(c_ident=ident, c_tril=tril, c_rc16=rc16, c_invw=invw, c_hsel=hsel)


_NC_CACHE = {}


def _run(inputs, n_cores, n_seq, seq_len, with_sample=True, trace=False):
    key = (n_seq, seq_len, with_sample)
    if key not in _NC_CACHE:
        _NC_CACHE[key] = build_program(n_seq, seq_len, with_sample)
    nc = _NC_CACHE[key]
    f = lambda a: np.ascontiguousarray(np.asarray(a, dtype=np.float32))
    cst = _consts()
    pm = np.concatenate([f(inputs["pool_scale"])[:, None, :], f(inputs["w_conf_dw"]), f(inputs["b_conf_dw"])[:, None, :],
                         f(inputs["conf_ln_g"])[:, None, :], f(inputs["conf_ln_b"])[:, None, :], f(inputs["w_sconv"])],
                        axis=1)
    gm = np.concatenate([np.concatenate([f(inputs["g_mix"])[l].reshape(KD, P), f(inputs["g_ffn"])[l].reshape(KD, P)], 0)
                         for l in range(DEPTH)], 0)
    wp = f(inputs["w_pool"])
    pw = np.zeros((P, DEPTH, 2, P), np.float32)
    for l in range(DEPTH):
        for g in range(4):
            j, hh = g // 2, g % 2
            pw[hh * 64:(hh + 1) * 64, l, j, hh * 64:(hh + 1) * 64] = wp[l, g]
    cpack = np.concatenate([cst["c_ident"], cst["c_tril"], cst["c_rc16"].reshape(P, 32), cst["c_invw"]], axis=1)
    shared = dict(
        g_final=f(inputs["g_final"]).reshape(1, D), w_in=f(inputs["w_in"]), w_s=f(inputs["w_s"]),
        w_out=f(inputs["w_out"]), w_gate=f(inputs["w_gate"]), w_up=f(inputs["w_up"]), w_down=f(inputs["w_down"]),
        c_pack=np.ascontiguousarray(cpack), c_hsel=cst["c_hsel"],
        pm_pack=np.ascontiguousarray(pm.transpose(1, 0, 2)), gm_pack=np.ascontiguousarray(gm),
        bs_pack=np.ascontiguousarray(np.concatenate([f(inputs["b_s"]).transpose(1, 0, 2)] * 2, axis=0)), pw_pack=pw)
    xp = f(inputs["x_prompt"])
    xsm = f(inputs["x_sample"])
    in_maps = []
    for c in range(n_cores):
        m = dict(shared)
        m["x_prompt"] = xp[c * n_seq:(c + 1) * n_seq]
        m["x_sample"] = xsm[c]
        sp_ = np.zeros((32, DEPTH, 3, WG), np.float32)
        sp_[0:15, :, 0, :] = f(inputs["state_pool"])[:, c].transpose(1, 0, 2)
        sp_[0:30, :, 1, :] = f(inputs["state_conv"])[:, c].transpose(1, 0, 2)
        sp_[0:2, :, 2, :] = f(inputs["state_short_conv"])[:, c].transpose(1, 0, 2)
        m["st_pack"] = sp_
        in_maps.append(m)
    res = run_bass_kernel_spmd(nc, in_maps, core_ids=list(range(n_cores)), trace=trace)
    R = res.results
    y_prompt = np.concatenate([r["y_prompt"] for r in R], axis=0)
    y_sample = np.stack([r["y_sample"] for r in R], axis=0)
    pool_p = np.concatenate([r["o_pool_p"] for r in R], axis=1)
    pool_s = np.stack([r["o_pool_s"] for r in R], axis=1)
    conv_p = np.concatenate([r["o_conv_p"] for r in R], axis=1)
    conv_s = np.stack([r["o_conv_s"] for r in R], axis=1)
    sc_p = np.concatenate([r["o_sc_p"] for r in R], axis=1)
    sc_s = np.stack([r["o_sc_s"] for r in R], axis=1)
    v_s = np.stack([r["o_v_s"] for r in R], axis=1)
    outs = (y_prompt, y_sample, pool_p, pool_s, conv_p, conv_s, sc_p, sc_s, v_s)
    if DEBUG:
        res.dbg = {k: np.asarray(R[0][k]).astype(np.float32) for k in R[0] if k.startswith('dbg_')}
    return tuple(np.ascontiguousarray(o, dtype=np.float32) for o in outs), res


def kernel(**inputs):
    outs, _ = _run(inputs, 8, 2, 4096, True)
    return outs
```
